# Optimizing a Trainium2 kernel written in Bass

```python
import jax, jax.numpy as jnp
from jax import lax
import numpy as np

D_MODEL = 1024
BATCH = 16
SEQ = 2048
DEPTH = 4

N_MIXERS = 2
D_FF = 2816
CONV_WIDTH = 31
DN_HEADS = 8
DN_HEAD_DIM = 128
DN_WIDTH = DN_HEADS * DN_HEAD_DIM
SHORT_CONV = 4
CHUNK = 64
N_CONV_LAYERS = (DEPTH + 1) // 2
N_DN_LAYERS = DEPTH // 2
N_SUB = 3
EPS = 1e-6
FFN_RES_WEIGHT = 0.5

kernel_name = "hybrid_conformer_gated_deltanet_block"


def rms_norm(x, g):
    xf = x.astype(jnp.float32)
    y = xf * lax.rsqrt(jnp.mean(xf * xf, axis=-1, keepdims=True) + EPS)
    return (y * g.astype(jnp.float32)).astype(x.dtype)


def layer_norm(x, g, b):
    xf = x.astype(jnp.float32)
    mu = jnp.mean(xf, axis=-1, keepdims=True)
    xc = xf - mu
    var = jnp.mean(xc * xc, axis=-1, keepdims=True)
    y = xc * lax.rsqrt(var + EPS) * g.astype(jnp.float32) + b.astype(jnp.float32)
    return y.astype(x.dtype)


def causal_dwconv(x, w):
    K, C = w.shape
    return lax.conv_general_dilated(
        x, w[:, None, :].astype(x.dtype), window_strides=(1,), padding=[(K - 1, 0)],
        dimension_numbers=("NWC", "WIO", "NWC"), feature_group_count=C)


def swiglu_ffn(h, w_in, w_out):
    gate, up = jnp.split(h @ w_in, 2, axis=-1)
    return (jax.nn.silu(gate) * up) @ w_out


def conv_module(h, w_glu, b_glu, w_dw, b_dw, ln_g, ln_b, w_pw, b_pw):
    a, b = jnp.split(h @ w_glu + b_glu, 2, axis=-1)
    u = a * jax.nn.sigmoid(b)
    u = causal_dwconv(u, w_dw) + b_dw
    u = jax.nn.silu(layer_norm(u, ln_g, ln_b))
    return u @ w_pw + b_pw


def l2norm(t):
    return t * lax.rsqrt(jnp.sum(t * t, axis=-1, keepdims=True) + EPS)


def chunk_gated_delta_rule(q, k, v, g, beta):
    f32 = jnp.float32
    B, T, H, Dh = q.shape
    N = T // CHUNK
    q = l2norm(q.astype(f32)) * (Dh ** -0.5)
    k = l2norm(k.astype(f32))
    v = v.astype(f32)

    def to_chunks(t):
        return t.reshape(B, N, CHUNK, H, -1).transpose(1, 0, 3, 2, 4)

    def to_chunks_s(t):
        return t.reshape(B, N, CHUNK, H).transpose(1, 0, 3, 2)

    q, k, v = to_chunks(q), to_chunks(k), to_chunks(v)
    beta = to_chunks_s(beta.astype(f32))
    g = jnp.cumsum(to_chunks_s(g.astype(f32)), axis=-1)

    causal = jnp.tril(jnp.ones((CHUNK, CHUNK), dtype=bool))
    strict = jnp.tril(jnp.ones((CHUNK, CHUNK), dtype=bool), -1)
    decay = jnp.exp(jnp.where(causal, g[..., :, None] - g[..., None, :], -jnp.inf))

    kb = k * beta[..., None]
    vb = v * beta[..., None]
    A = jnp.where(strict, jnp.einsum("nbhid,nbhjd->nbhij", kb, k) * decay, 0.0)
    eye = jnp.broadcast_to(jnp.eye(CHUNK, dtype=f32), A.shape)
    Tm = lax.linalg.triangular_solve(A, eye, left_side=True, lower=True, unit_diagonal=True)

    u = jnp.einsum("nbhij,nbhjd->nbhid", Tm, vb)
    w = jnp.einsum("nbhij,nbhjd->nbhid", Tm, kb * jnp.exp(g)[..., None])
    qg = q * jnp.exp(g)[..., None]
    intra = jnp.einsum("nbhid,nbhjd->nbhij", q, k) * decay
    g_last = g[..., -1]
    kd = k * jnp.exp(g_last[..., None] - g)[..., None]

    def step(S, inp):
        qg_i, w_i, u_i, intra_i, kd_i, gl_i = inp
        v_new = u_i - jnp.einsum("bhcd,bhde->bhce", w_i, S)
        o_i = jnp.einsum("bhcd,bhde->bhce", qg_i, S) + jnp.einsum("bhij,bhje->bhie", intra_i, v_new)
        S = S * jnp.exp(gl_i)[..., None, None] + jnp.einsum("bhcd,bhce->bhde", kd_i, v_new)
        return S, o_i

    S0 = jnp.zeros((B, H, Dh, v.shape[-1]), dtype=f32)
    _, o = lax.scan(step, S0, (qg, w, u, intra, kd, g_last))
    return o.transpose(1, 0, 3, 2, 4).reshape(B, T, H, -1)


def gated_deltanet(h, w_in, w_sconv, a_log, dt_bias, o_g, w_out):
    B, T, _ = h.shape
    W, H = DN_WIDTH, DN_HEADS
    proj = h @ w_in
    qkv = proj[..., :3 * W]
    z = proj[..., 3 * W:4 * W]
    a = proj[..., 4 * W:4 * W + H]
    b = proj[..., 4 * W + H:]
    qkv = jax.nn.silu(causal_dwconv(qkv, w_sconv))
    q, k, v = [t.reshape(B, T, H, DN_HEAD_DIM) for t in jnp.split(qkv, 3, axis=-1)]
    g = -jnp.exp(a_log.astype(jnp.float32)) * jax.nn.softplus(a.astype(jnp.float32) + dt_bias.astype(jnp.float32))
    beta = jax.nn.sigmoid(b.astype(jnp.float32))
    o = chunk_gated_delta_rule(q, k, v, g, beta).astype(h.dtype)
    o = rms_norm(o, o_g) * jax.nn.silu(z.reshape(B, T, H, DN_HEAD_DIM))
    return o.reshape(B, T, W) @ w_out


def setup_inputs(seed: int = 0) -> dict:
    key = jax.random.key(seed)
    ks = iter(jax.random.split(key, 32))
    f32 = jnp.float32
    D, F, W, H = D_MODEL, D_FF, DN_WIDTH, DN_HEADS
    NA, NB = N_CONV_LAYERS, N_DN_LAYERS

    def nrm(shape, fan_in, mult=1.0):
        return jax.random.normal(next(ks), shape, f32) * (mult * fan_in ** -0.5)

    def small(shape, s=0.02):
        return jax.random.normal(next(ks), shape, f32) * s

    x = jax.random.normal(next(ks), (BATCH, SEQ, D), f32)
    c = jax.random.normal(next(ks), (BATCH, D), f32)
    norm_g = 1.0 + small((DEPTH, N_SUB, D))
    w_ada = nrm((DEPTH, D, N_SUB * 3 * D), D, 0.2)
    b_ada = small((DEPTH, N_SUB * 3 * D))
    w_ffn_in = nrm((DEPTH, 2, D, 2 * F), D)
    w_ffn_out = nrm((DEPTH, 2, F, D), F)
    cm_w_glu = nrm((NA, D, 2 * D), D)
    cm_b_glu = small((NA, 2 * D))
    cm_w_dw = nrm((NA, CONV_WIDTH, D), CONV_WIDTH)
    cm_b_dw = small((NA, D))
    cm_ln_g = 1.0 + small((NA, D))
    cm_ln_b = small((NA, D))
    cm_w_pw = nrm((NA, D, D), D)
    cm_b_pw = small((NA, D))
    dn_w_in = nrm((NB, D, 4 * W + 2 * H), D)
    dn_w_sconv = nrm((NB, SHORT_CONV, 3 * W), SHORT_CONV)
    dn_a_log = jnp.log(jax.random.uniform(next(ks), (NB, H), f32, 1.0, 16.0))
    dt = jnp.exp(jax.random.uniform(next(ks), (NB, H), f32, np.log(1e-3), np.log(1e-1)))
    dn_dt_bias = dt + jnp.log(-jnp.expm1(-dt))
    dn_o_g = 1.0 + small((NB, DN_HEAD_DIM))
    dn_w_out = nrm((NB, W, D), W)
    final_g = 1.0 + small((D,))
    return {"x": x, "c": c, "norm_g": norm_g, "w_ada": w_ada, "b_ada": b_ada,
            "w_ffn_in": w_ffn_in, "w_ffn_out": w_ffn_out,
            "cm_w_glu": cm_w_glu, "cm_b_glu": cm_b_glu, "cm_w_dw": cm_w_dw, "cm_b_dw": cm_b_dw,
            "cm_ln_g": cm_ln_g, "cm_ln_b": cm_ln_b, "cm_w_pw": cm_w_pw, "cm_b_pw": cm_b_pw,
            "dn_w_in": dn_w_in, "dn_w_sconv": dn_w_sconv, "dn_a_log": dn_a_log,
            "dn_dt_bias": dn_dt_bias, "dn_o_g": dn_o_g, "dn_w_out": dn_w_out,
            "final_g": final_g}


def reference(x, c, norm_g, w_ada, b_ada, w_ffn_in, w_ffn_out,
              cm_w_glu, cm_b_glu, cm_w_dw, cm_b_dw, cm_ln_g, cm_ln_b, cm_w_pw, cm_b_pw,
              dn_w_in, dn_w_sconv, dn_a_log, dn_dt_bias, dn_o_g, dn_w_out, final_g):
    B = x.shape[0]
    D = x.shape[-1]
    cs = jax.nn.silu(c)
    for i in range(DEPTH):
        mod = (cs @ w_ada[i] + b_ada[i]).reshape(B, N_SUB, 3, D)

        def modulated(x, j):
            shift, scale = mod[:, j, 0][:, None, :], mod[:, j, 1][:, None, :]
            return rms_norm(x, norm_g[i, j]) * (1.0 + scale) + shift

        def gate(j):
            return 1.0 + mod[:, j, 2][:, None, :]

        h = modulated(x, 0)
        x = x + FFN_RES_WEIGHT * gate(0) * swiglu_ffn(h, w_ffn_in[i, 0], w_ffn_out[i, 0])

        h = modulated(x, 1)
        if i % N_MIXERS == 0:
            a = i // N_MIXERS
            y = conv_module(h, cm_w_glu[a], cm_b_glu[a], cm_w_dw[a], cm_b_dw[a],
                            cm_ln_g[a], cm_ln_b[a], cm_w_pw[a], cm_b_pw[a])
        else:
            m = i // N_MIXERS
            y = gated_deltanet(h, dn_w_in[m], dn_w_sconv[m], dn_a_log[m], dn_dt_bias[m],
                               dn_o_g[m], dn_w_out[m])
        x = x + gate(1) * y

        h = modulated(x, 2)
        x = x + FFN_RES_WEIGHT * gate(2) * swiglu_ffn(h, w_ffn_in[i, 1], w_ffn_out[i, 1])
    return rms_norm(x, final_g)
```

```python
import contextlib
import numpy as np
import concourse.bass as bass
import concourse.mybir as mybir
from concourse.bass_utils import run_bass_kernel_spmd

F32 = mybir.dt.float32
BF16 = mybir.dt.bfloat16
AF = mybir.ActivationFunctionType
ALU = mybir.AluOpType

D = 1024
T = 2048
KC = 8
TT = 512
NT = T // TT
FF = 2816
FC = 22
DEPTH = 4
EPS = 1e-6
NCORES = 8
ENGS = ("pe", "act", "dve", "pool", "sp")
F32R = mybir.dt.float32r
_ESZ = {F32: 4, BF16: 2, F32R: 4}


def _res(ap):
    a = ap.ap
    ps = a[0][0]
    off = ap.offset % ps if ps else ap.offset
    span = 1
    for s, n in a[1:]:
        span += (n - 1) * abs(s)
    e = _ESZ[ap.dtype]
    if "PSUM" in str(ap.space).upper():
        return ("@" + ap.tensor.name, 0, 1 << 20)
    return (ap.tensor.name, off * e, (off + span) * e)


class Sched:
    NDMA = 6

    def __init__(self, nc, stack):
        self.nc = nc
        self.sems = {}
        for e in ENGS:
            self.sems[("p", e)] = stack.enter_context(nc.semaphore(f"prog_{e}"))
        for q in ("sp", "pool"):
            for i in range(self.NDMA):
                self.sems[("d", q, i)] = stack.enter_context(nc.semaphore(f"dma_{q}_{i}"))
        self.cnt = {k: 0 for k in self.sems}
        self.seen = {e: {} for e in ENGS}
        self.prog = {e: [] for e in ENGS}
        self.ents = {}
        self.dma_i = {"sp": 0, "pool": 0}
        self.dma_last = {}
        self.nops = {e: 0 for e in ENGS}
        self.label = ''
        self.labels = {e: [] for e in ENGS}

    def _collect(self, e, reads, writes, extra=()):
        need = {}

        def add(tok):
            if tok is not None and need.get(tok[0], 0) < tok[1]:
                need[tok[0]] = tok[1]
        own = ("p", e)
        for (name, lo, hi) in reads:
            for ent in self.ents.get(name, ()):
                if ent[0] < hi and lo < ent[1]:
                    add(ent[2])
                    if name[0] == "@":
                        for k, v in ent[3].items():
                            if k != own:
                                add((k, v))
        for (name, lo, hi) in writes:
            for ent in self.ents.get(name, ()):
                if ent[0] < hi and lo < ent[1]:
                    add(ent[2])
                    for k, v in ent[3].items():
                        add((k, v))
        for t in extra:
            add(t)
        waits = []
        seen = self.seen[e]
        for k, v in need.items():
            if e == "pe" and k == ("p", "pe"):
                continue
            if seen.get(k, 0) >= v:
                continue
            seen[k] = v
            waits.append((k, v))
        return waits

    def _commit(self, reads, writes, tok):
        k, v = tok
        for (name, lo, hi) in reads:
            ents = self.ents.setdefault(name, [])
            covered = False
            for ent in ents:
                if ent[0] < hi and lo < ent[1]:
                    if ent[3].get(k, 0) < v:
                        ent[3][k] = v
                    if ent[0] <= lo and hi <= ent[1]:
                        covered = True
            if not covered:
                ents.append([lo, hi, None, {k: v}])
        for (name, lo, hi) in writes:
            ents = self.ents.setdefault(name, [])
            new = [ent for ent in ents
                   if not (ent[0] < hi and lo < ent[1] and lo <= ent[0] and ent[1] <= hi)]
            new.append([lo, hi, tok, {}])
            self.ents[name] = new

    def op(self, e, fn, reads=(), writes=(), signal=True):
        reads = [_res(a) for a in reads]
        writes = [_res(a) for a in writes]
        waits = self._collect(e, reads, writes)
        pk = ("p", e)
        if signal:
            self.cnt[pk] += 1
            tok = (pk, self.cnt[pk])
        else:
            tok = (pk, self.cnt[pk] + 1)
        self._commit(reads, writes, tok)
        sems = self.sems
        self.nops[e] += 1
        self.labels[e].append(self.label)

        def emit(engine, waits=waits, fn=fn, signal=signal, pk=pk):
            for k, v in waits:
                engine.wait_ge(sems[k], v)
            ins = fn(engine)
            if signal:
                ins.then_inc(sems[pk], 1)
        self.prog[e].append(emit)
        return tok

    def dma(self, q, out, in_, **kw):
        reads = [_res(in_)] if _is_onchip(in_) else []
        writes = [_res(out)] if _is_onchip(out) else []
        i = self.dma_i[q] % self.NDMA
        self.dma_i[q] += 1
        dk = ("d", q, i)
        extra = []
        if dk in self.dma_last:
            extra.append(self.dma_last[dk])
        waits = self._collect(q, reads, writes, extra)
        self.cnt[dk] += 16
        tok = (dk, self.cnt[dk])
        self.dma_last[dk] = tok
        self._commit(reads, writes, tok)
        sems = self.sems
        self.nops[q] += 1

        def emit(engine, waits=waits, out=out, in_=in_, dk=dk, kw=kw):
            for k, v in waits:
                engine.wait_ge(sems[k], v)
            engine.dma_start(out=out, in_=in_, **kw).then_inc(sems[dk], 16)
        self.prog[q].append(emit)
        return tok

    def wait_tokens(self, e, toks):
        sems = self.sems
        ws = []
        for k, v in toks:
            if self.seen[e].get(k, 0) >= v:
                continue
            self.seen[e][k] = v
            ws.append((k, v))

        def emit(engine, ws=ws):
            for k, v in ws:
                engine.wait_ge(sems[k], v)
        self.prog[e].append(emit)

    def replay(self, block):
        prog = self.prog

        @block.sync
        def _(eng):
            for f in prog["sp"]:
                f(eng)

        @block.scalar
        def _(eng):
            for f in prog["act"]:
                f(eng)

        @block.vector
        def _(eng):
            for f in prog["dve"]:
                f(eng)

        @block.gpsimd
        def _(eng):
            for f in prog["pool"]:
                f(eng)

        @block.tensor
        def _(eng):
            for f in prog["pe"]:
                f(eng)

    def mm(self, out, lhsT, rhs, start=True, stop=True, lazy=False):
        self.op("pe", lambda e: e.matmul(out, lhsT=lhsT, rhs=rhs, start=start, stop=stop),
                reads=[lhsT, rhs], writes=[out], signal=(stop or not lazy))

    def transpose(self, out, in_, ident):
        self.op("pe", lambda e: e.transpose(out, in_, ident), reads=[in_, ident], writes=[out])

    def act(self, out, in_, func, scale=1.0, bias=0.0):
        rd = [in_]
        if not isinstance(scale, (int, float)):
            rd.append(scale)
        if not isinstance(bias, (int, float)):
            rd.append(bias)
        self.op("act", lambda e: e.activation(out=out, in_=in_, func=func, scale=scale, bias=bias),
                reads=rd, writes=[out])

    def tt(self, out, in0, in1, op, eng="dve"):
        self.op(eng, lambda e: e.tensor_tensor(out=out, in0=in0, in1=in1, op=op),
                reads=[in0, in1], writes=[out])

    def ts(self, out, in0, s1, op0, s2=None, op1=None, eng="dve"):
        rd = [in0]
        if not isinstance(s1, (int, float)):
            rd.append(s1)
        if s2 is not None and not isinstance(s2, (int, float)):
            rd.append(s2)
        if op1 is None:
            self.op(eng, lambda e: e.tensor_scalar(out=out, in0=in0, scalar1=s1, scalar2=None, op0=op0),
                    reads=rd, writes=[out])
        else:
            self.op(eng, lambda e: e.tensor_scalar(out=out, in0=in0, scalar1=s1, scalar2=s2, op0=op0, op1=op1),
                    reads=rd, writes=[out])

    def stt(self, out, in0, scalar, in1, op0, op1):
        rd = [in0, in1]
        if not isinstance(scalar, (int, float)):
            rd.append(scalar)
        self.op("dve", lambda e: e.scalar_tensor_tensor(out=out, in0=in0, scalar=scalar, in1=in1, op0=op0, op1=op1),
                reads=rd, writes=[out])

    def copy(self, out, in_, eng="dve"):
        if eng == "act":
            self.op("act", lambda e: e.copy(out=out, in_=in_), reads=[in_], writes=[out])
        else:
            self.op(eng, lambda e: e.tensor_copy(out=out, in_=in_), reads=[in_], writes=[out])

    def memset(self, ap, val, eng="dve"):
        self.op(eng, lambda e: e.memset(ap, val), reads=[], writes=[ap])


def _is_onchip(ap):
    return "DRAM" not in str(ap.space).upper()


class Prog:
    def __init__(self, stages=None, nseq=2, do_final=True, debug=False):
        self.debug = debug
        if stages is None:
            stages = [(i, j) for i in range(DEPTH) for j in range(3)]
        self.stages = list(stages)
        self.layers = sorted(set(i for i, _ in self.stages))
        self.nseq = nseq
        self.do_final = do_final
        self.nc = bass.Bass("TRN2", target_bir_lowering=False)
        self.prenormed = {}
        self._force = None
        self._build()

    def _dram(self):
        nc = self.nc

        def din(name, shape):
            return nc.dram_tensor(name, list(shape), F32, kind="ExternalInput").ap()
        d = {}
        d["xT"] = din("xT", [2, D, T])
        d["cT"] = din("cT", [128, KC * 2])
        NL = len(self.layers)
        d["ng"] = din("ng", [128, DEPTH * 3 * KC])
        d["wada"] = din("wada", [NL, 72, 128, KC * 128])
        d["bada"] = din("bada", [128, DEPTH * 72])
        d["win"] = din("win", [NL, 2, FC, 128, KC * 256])
        d["wout"] = din("wout", [NL, 2, 2, KC, 128, 11 * 128])
        d["fg"] = din("fg", [128, KC])
        d["cmglu"] = din("cmglu", [2, KC, 128, KC * 256])
        d["cmpw"] = din("cmpw", [2, KC, 128, KC * 128])
        d["cmsm"] = din("cmsm", [128, 2 * 296])
        d["dnwin"] = din("dnwin", [2, 8, 2, 128, KC * 256])
        d["dnwab"] = din("dnwab", [2, 128, KC * 16])
        d["dnwout"] = din("dnwout", [2, 8, 128, D])
        d["dnsm"] = din("dnsm", [128, 2 * 97])
        d["dnhp"] = din("dnhp", [8, 4])
        d["maskc"] = din("maskc", [128, 3 * 128])
        d["sel127"] = din("sel127", [128, 128])
        d["cmask"] = din("cmask", [8, T])
        d["identf"] = din("identf", [128, 128])
        self.out = nc.dram_tensor("outT", [2, D, T], F32, kind="ExternalOutput").ap()
        self.dbg = nc.dram_tensor("dbg", [128, 8192], F32, kind="ExternalOutput").ap() if self.debug else None
        self.d = d

    def _build(self):
        nc = self.nc
        self._dram()
        with contextlib.ExitStack() as st:
            def sb(name, shape, dt):
                return st.enter_context(nc.sbuf_tensor(name, list(shape), dt))
            self.xs = sb("xs", [128, KC, T], F32)
            self.hb = sb("hb", [128, KC, T], BF16)
            self.SCR_BYTES = 65 * 1024
            self.scr = sb("scr", [128, self.SCR_BYTES // 4], F32)
            self.scr_bf = self.scr[:, :].bitcast(BF16)
            self.slots = sb("slots", [128, 5 * TT], F32)
            self.ot0 = sb("ot0", [128, TT], F32)
            self.mod = sb("mod", [128, DEPTH, 2, 72], F32)
            self.msc = sb("msc", [128, DEPTH, 2, 3, 3, KC], F32)
            self.ct = sb("ct", [128, KC * 2], F32)
            self.cs = sb("cs", [128, KC * 2], F32)
            self.ng = sb("ngs", [128, DEPTH * 3 * KC], F32)
            self.bada = sb("badas", [128, DEPTH * 72], F32)
            self.fg = sb("fgs", [128, KC], F32)
            self.identf = sb("identfs", [128, 128], F32)
            self.onesb = sb("onesb", [128, 128], BF16)
            self.identb = sb("identb", [128, 128], BF16)
            self.cmsm = sb("cmsms", [128, 2 * 296], F32)
            self.cbg = sb("cbg", [128, KC], F32)
            self.dnsm = sb("dnsms", [128, 2 * 97], F32)
            self.dnhp = sb("dnhps", [8, 4], F32)
            self.maskc = sb("maskcs", [128, 3 * 128], F32)
            self.sel127 = sb("sel127s", [128, 128], F32)
            self.onesf = sb("onesf", [128, 128], F32)
            self.nonesf = sb("nonesf", [128, 128], F32)
            self.nexpa = sb("nexpa", [8, 1], F32)
            self.gtok = sb("gtok", [128, 128], F32)
            self.btok = sb("btok", [128, 128], F32)
            self.ekd = sb("ekd", [128, 128], F32)
            self.bkg = sb("bkg", [128, 128], F32)
            self.egl = sb("egl", [128, 16], F32)
            self.Sst = sb("Sst", [128, 128], F32)
            self.Sb = [sb(f"Sb{i}", [128, 128], BF16) for i in range(2)]
            self.vnew = [sb(f"vnew{i}", [128, 128], BF16) for i in range(2)]
            self.sqb = [sb(f"sqb{i}", [128, TT], BF16) for i in range(2)]
            self.sq = [sb(f"sq{i}", [128, TT], BF16) for i in range(2)]
            self.rstd = [sb(f"rstd{i}", [128, TT], F32) for i in range(2)]
            self.xn = [sb(f"xn{i}", [128, TT], F32) for i in range(2)]
            self.sg = [sb(f"sg{i}", [128, TT], F32) for i in range(2)]
            self.ps = [st.enter_context(nc.psum_tensor(f"ps{i}", [128, TT], F32)) for i in range(8)]
            self.S = Sched(nc, st)
            block = st.enter_context(nc.Block())
            self._emit_all()
            self.S.replay(block)

    def scr_f32(self, byte_off, shape):
        n = int(np.prod(shape))
        v = self.scr[:, byte_off // 4: byte_off // 4 + n]
        return self._shape(v, shape)

    def scr_b16(self, byte_off, shape):
        n = int(np.prod(shape))
        v = self.scr_bf[:, byte_off // 2: byte_off // 2 + n]
        return self._shape(v, shape)

    @staticmethod
    def _shape(v, shape):
        if len(shape) == 1:
            return v
        if len(shape) == 2:
            return v.rearrange("p (a b) -> p a b", a=shape[0])
        if len(shape) == 3:
            return v.rearrange("p (a b c) -> p a b c", a=shape[0], b=shape[1])
        raise ValueError

    def _emit_all(self):
        S = self.S
        d = self.d
        S.dma("sp", self.ct[:], d["cT"][:, :])
        S.dma("sp", self.ng[:], d["ng"][:, :])
        S.dma("sp", self.bada[:], d["bada"][:, :])
        S.dma("sp", self.fg[:], d["fg"][:, :])
        S.dma("sp", self.identf[:], d["identf"][:, :])
        S.memset(self.onesb[:], 1.0)
        S.dma("sp", self.cmsm[:], d["cmsm"][:, :])
        S.copy(self.identb[:], self.identf[:])
        S.dma("sp", self.dnsm[:], d["dnsm"][:, :])
        S.dma("sp", self.dnhp[:], d["dnhp"][:, :])
        S.dma("sp", self.maskc[:], d["maskc"][:, :])
        S.dma("sp", self.sel127[:], d["sel127"][:, :])
        S.memset(self.onesf[:], 1.0)
        S.memset(self.nonesf[:], -1.0)
        S.act(self.cs[:], self.ct[:], AF.Silu)
        self._adaln()
        toks = []
        for s in range(self.nseq):
            self._load_x(s)
            for si, (i, j) in enumerate(self.stages):
                nxt = self.stages[si + 1] if si + 1 < len(self.stages) else None
                nn = None
                if j in (0, 2) and nxt is not None and nxt[1] in (0, 1, 2):
                    ni, nj = nxt

                    def nn(tts, ni=ni, nj=nj, s=s):
                        self._norm_mod(ni, s, nj, tts=self._mk(tts))
                        self.prenormed[(ni, s, nj)] = True
                if j == 0:
                    self._ffn(i, s, 0, nn)
                elif j == 1:
                    self._mixer(i, s)
                elif j == 2:
                    self._ffn(i, s, 1, nn)
                elif j == 8:
                    self._norm_mod(i, s, 0)
            if self.debug and s == 0:
                toks += self._dump()
            toks += self._finish(s)
        S.wait_tokens("sp", toks)

    class _Forced(tuple):
        pass

    def _mk(self, tts):
        f = Prog._Forced(tts)
        self._force = f
        return f

    def _load_x(self, s):
        for kc in range(KC):
            self.S.dma("sp", self.xs[:, kc, :], self.d["xT"][s, kc * 128:(kc + 1) * 128, :])

    def _adaln(self):
        S = self.S
        bufs = [self.scr_f32(0, [4, KC * 128]), self.scr_f32(16384, [4, KC * 128])]
        for i in self.layers:
            ps = self.ps[7]
            psv = ps[:, 0:144].rearrange("p (s c) -> p c s", s=2)
            for g in range(18):
                buf = bufs[g % 2]
                S.dma("sp", buf, self.d["wada"][self.layers.index(i), g * 4:(g + 1) * 4].rearrange("c p n -> p c n"))
                for j in range(4):
                    ch = g * 4 + j
                    for kc in range(KC):
                        S.mm(psv[:, ch, :], buf[:, j, kc * 128:(kc + 1) * 128],
                             self.cs[:, kc * 2:(kc + 1) * 2], start=(kc == 0), stop=(kc == KC - 1))
            for s in range(2):
                S.tt(self.mod[:, i, s, :], ps[:, s * 72:(s + 1) * 72], self.bada[:, i * 72:(i + 1) * 72], ALU.add)
            for s in range(2):
                for j in range(3):
                    m = self.mod[:, i, s, j * 24:(j + 1) * 24]
                    ngv = self.ng[:, (i * 3 + j) * KC:(i * 3 + j + 1) * KC]
                    S.stt(self.msc[:, i, s, j, 0, :], m[:, 8:16], 1.0, ngv, ALU.add, ALU.mult)
                    S.copy(self.msc[:, i, s, j, 1, :], m[:, 0:8])
                    rw = 1.0 if j == 1 else 0.5
                    S.ts(self.msc[:, i, s, j, 2, :], m[:, 16:24], 1.0, ALU.add, rw, ALU.mult)

    def _norm(self, gs_fn, shift_fn, out_fn, tts=range(NT)):
        S = self.S
        for tt in tts:
            sl = slice(tt * TT, (tt + 1) * TT)
            ps = self.ps[6 + (tt % 2)]
            for kc in range(KC):
                if kc % 2 == 0:
                    sq = self.sq[(kc // 2) % 2]
                    S.act(sq[:], self.xs[:, kc, sl], AF.Square)
                else:
                    sq = self.sqb[(kc // 2) % 2]
                    S.tt(sq[:], self.xs[:, kc, sl], self.xs[:, kc, sl], ALU.mult, eng="pool")
                S.mm(ps[:], self.onesb[:], sq[:], start=(kc == 0), stop=(kc == KC - 1))
            r = self.rstd[tt % 2]
            S.act(r[:], ps[:], AF.Ln, scale=1.0 / D, bias=EPS)
            S.act(ps[:], r[:], AF.Exp, scale=-0.5)
            for kc in range(KC):
                xn = self.xn[kc % 2]
                S.tt(xn[:], self.xs[:, kc, sl], ps[:], ALU.mult)
                S.act(out_fn(kc, tt), xn[:], AF.Identity, scale=gs_fn(kc), bias=shift_fn(kc))

    def _norm_mod(self, i, s, j, tts=range(NT)):
        if self.prenormed.get((i, s, j)) and tts is not self._force:
            return
        self._norm(lambda kc: self.msc[:, i, s, j, 0, kc:kc + 1],
                   lambda kc: self.msc[:, i, s, j, 1, kc:kc + 1],
                   lambda kc, tt: self.hb[:, kc, tt * TT:(tt + 1) * TT], tts=tts)

    def _ffn(self, i, s, which, next_norm=None):
        S = self.S
        j = 0 if which == 0 else 2
        self._norm_mod(i, s, j)
        hid = self.scr_b16(0, [11, T])
        winb = [self.scr_b16(45056 + b * 4096, [KC, 256]) for b in range(3)]
        woutb = [self.scr_b16(45056 + 12288 + b * 2816, [11, 128]) for b in range(2)]
        win_d = self.d["win"]
        wout_d = self.d["wout"]
        gatec = lambda kc: self.msc[:, i, s, j, 2, kc:kc + 1]
        n_in = 0
        n_out = 0
        for half in range(2):
            for f in range(11):
                fc = half * 11 + f
                wb = winb[n_in % 3]
                n_in += 1
                S.dma("pool", wb, win_d[self.layers.index(i), which, fc].rearrange("p (k n) -> p k n", k=KC))
                for tt in range(NT):
                    sl = slice(tt * TT, (tt + 1) * TT)
                    pg = self.ps[(f * NT + tt) % 2]
                    pu = self.ps[2 + (f * NT + tt) % 2]
                    for kc in range(KC):
                        S.mm(pg[:], wb[:, kc, 0:128], self.hb[:, kc, sl], start=(kc == 0), stop=(kc == KC - 1), lazy=True)
                    for kc in range(KC):
                        S.mm(pu[:], wb[:, kc, 128:256], self.hb[:, kc, sl], start=(kc == 0), stop=(kc == KC - 1), lazy=True)
                    sg = self.sg[(f * NT + tt) % 2]
                    S.act(sg[:], pg[:], AF.Silu)
                    S.tt(hid[:, f, sl], pu[:], sg[:], ALU.mult)
            split = (half == 1 and next_norm is not None)
            groups = [(0, 1), (2, 3)] if split else [(0, 1, 2, 3)]
            for grp in groups:
                for dc in range(KC):
                    wo = woutb[n_out % 2]
                    n_out += 1
                    S.dma("pool", wo, wout_d[self.layers.index(i), which, half, dc].rearrange("p (k n) -> p k n", k=11))
                    for tt in grp:
                        sl = slice(tt * TT, (tt + 1) * TT)
                        po = self.ps[4 + (dc * NT + tt) % 2]
                        for f in range(11):
                            S.mm(po[:], wo[:, f, :], hid[:, f, sl], start=(f == 0), stop=(f == 10), lazy=True)
                        S.stt(self.xs[:, dc, sl], po[:], gatec(dc), self.xs[:, dc, sl], ALU.mult, ALU.add)
                if split:
                    next_norm(grp)

    def _mixer(self, i, s):
        if i % 2 == 0:
            self._conv_mixer(i, s)
        else:
            self._dn_mixer(i, s)

    def _conv_mixer(self, i, s):
        S = self.S
        a = i // 2
        j = 1
        S.label = 'cv_norm'
        self._norm_mod(i, s, j)
        S.label = 'cv_glu'
        base = a * 296
        sm = self.cmsm
        bga = lambda oc: sm[:, base + oc:base + oc + 1]
        bgb = lambda oc: sm[:, base + 8 + oc:base + 8 + oc + 1]
        wdw = lambda oc, k: sm[:, base + 16 + oc * 31 + k:base + 16 + oc * 31 + k + 1]
        bdw = lambda oc: sm[:, base + 264 + oc:base + 264 + oc + 1]
        lng = lambda oc: sm[:, base + 272 + oc:base + 272 + oc + 1]
        lnb = lambda oc: sm[:, base + 280 + oc:base + 280 + oc + 1]
        gate = self.msc[:, i, s, j, 2, :]
        S.tt(self.cbg[:], sm[:, base + 288:base + 296], gate, ALU.mult)
        UW = 30 + T
        u = self.scr_b16(0, [KC, UW])
        wgl = [self.scr_b16(33280 + b * 4096, [KC, 256]) for b in range(2)]
        wpw = [self.scr_b16(33280 + b * 2048, [KC, 128]) for b in range(2)]
        diag = [self.scr_b16(33280, [31, 128])]
        sgt = [self.sg[0][:], self.sg[1][:]]
        u2 = self.scr_b16(41472, [KC, 1024])
        yv = self.hb[:, :, :].rearrange("p a b -> p (a b)").bitcast(F32)
        y = yv.rearrange("p (a b) -> p a b", a=KC)
        S.memset(u[:, :, 0:30], 0.0)
        for oc in range(KC):
            wb = wgl[oc % 2]
            S.dma("pool", wb, self.d["cmglu"][a, oc].rearrange("p (k n) -> p k n", k=KC))
            for tt in range(NT):
                sl = slice(tt * TT, (tt + 1) * TT)
                pa = self.ps[(oc * NT + tt) % 2]
                pb = self.ps[2 + (oc * NT + tt) % 2]
                for kc in range(KC):
                    S.mm(pa[:], wb[:, kc, 0:128], self.hb[:, kc, sl], start=(kc == 0), stop=(kc == KC - 1), lazy=True)
                for kc in range(KC):
                    S.mm(pb[:], wb[:, kc, 128:256], self.hb[:, kc, sl], start=(kc == 0), stop=(kc == KC - 1), lazy=True)
                sg = sgt[(oc * NT + tt) % 2]
                S.act(sg, pb[:], AF.Sigmoid, bias=bgb(oc))
                S.stt(u[:, oc, 30 + tt * TT:30 + (tt + 1) * TT], pa[:], bga(oc), sg, ALU.add, ALU.mult)
        npw = 0
        for hf in range(2):
            S.label = f'cv_conv{hf}'
            for oc in range(KC):
                dg = diag[0]
                for k in range(31):
                    S.ts(dg[:, k, :], self.identb[:], wdw(oc, k), ALU.mult)
                for t2 in range(2):
                    tt = hf * 2 + t2
                    pc = self.ps[4 + (oc * 2 + t2) % 2]
                    for k in range(31):
                        S.mm(pc[:], dg[:, k, :], u[:, oc, tt * TT + k:tt * TT + k + TT],
                             start=(k == 0), stop=(k == 30), lazy=True)
                    S.act(y[:, oc, t2 * TT:(t2 + 1) * TT], pc[:], AF.Identity, bias=bdw(oc))
            S.label = f'cv_ln{hf}'
            for t2 in range(2):
                ysl = slice(t2 * TT, (t2 + 1) * TT)
                psA = self.ps[6]
                psB = self.ps[7]
                for kc in range(KC):
                    qa = self.sq[kc % 2]
                    qb = self.sqb[kc % 2]
                    S.copy(qa[:], y[:, kc, ysl], eng="act")
                    S.act(qb[:], y[:, kc, ysl], AF.Square)
                    S.mm(psA[:], self.onesb[:], qa[:], start=(kc == 0), stop=(kc == KC - 1))
                    S.mm(psB[:], self.onesb[:], qb[:], start=(kc == 0), stop=(kc == KC - 1))
                m = self.rstd[0]
                r = self.rstd[1]
                S.act(m[:], psA[:], AF.Identity, scale=1.0 / D)
                S.tt(r[:], m[:], m[:], ALU.mult)
                S.stt(r[:], psB[:], 1.0 / D, r[:], ALU.mult, ALU.subtract)
                S.act(r[:], r[:], AF.Ln, bias=EPS)
                S.act(r[:], r[:], AF.Exp, scale=-0.5)
                for kc in range(KC):
                    xn = self.xn[kc % 2]
                    S.tt(xn[:], y[:, kc, ysl], m[:], ALU.subtract)
                    S.tt(xn[:], xn[:], r[:], ALU.mult)
                    S.act(u2[:, kc, ysl], xn[:], AF.Silu, scale=lng(kc), bias=lnb(kc))
            S.label = f'cv_pw{hf}'
            for oc in range(KC):
                wp = wpw[npw % 2]
                npw += 1
                S.dma("pool", wp, self.d["cmpw"][a, oc].rearrange("p (k n) -> p k n", k=KC))
                for t2 in range(2):
                    tt = hf * 2 + t2
                    sl = slice(tt * TT, (tt + 1) * TT)
                    po = self.ps[(oc * 2 + t2) % 2]
                    for kc in range(KC):
                        S.mm(po[:], wp[:, kc, :], u2[:, kc, t2 * TT:(t2 + 1) * TT],
                             start=(kc == 0), stop=(kc == KC - 1), lazy=True)
                    tmp = self.sg[(oc * 2 + t2) % 2]
                    S.act(tmp[:], po[:], AF.Identity, scale=gate[:, oc:oc + 1], bias=self.cbg[:, oc:oc + 1])
                    S.tt(self.xs[:, oc, sl], self.xs[:, oc, sl], tmp[:], ALU.add)

    def _dn_mixer(self, i, s):
        S = self.S
        m = i // 2
        j = 1
        S.label = 'dn_norm'
        self._norm_mod(i, s, j)
        ps = self.ps
        gate = self.msc[:, i, s, j, 2, :]
        sm = self.dnsm
        sconv = lambda c24, k: sm[:, m * 97 + c24 * 4 + k:m * 97 + c24 * 4 + k + 1]
        og = sm[:, m * 97 + 96:m * 97 + 97]
        NTL = 16
        RG, R2, RC, RV, RW, RK, RU = 0, 4096, 16384, 20544, 32832, 41024, 53312
        gz = self.scr_b16(RG, [T])
        tmpf = lambda k: self.scr_f32(R2 + k * 2048, [TT])
        cf = self.scr_f32(R2, [T])
        dgs = self.scr_b16(R2 + 4 * 2048, [4, 128])
        cin = self.scr_b16(RC, [3 + T])
        vb = self.scr_b16(RV, [NTL, 128])
        kbg = self.scr_b16(RV + 4096, [NTL, 128])
        kd = self.scr_b16(RV + 8192, [NTL, 128])
        wqk = self.scr_b16(RW, [KC, 256])
        wvz = self.scr_b16(RW + 4096, [KC, 256])
        Rb = self.scr_b16(RW, [T])
        intraT = self.scr_b16(RW + 4096, [T])
        kTb = self.scr_b16(RK, [T])
        qTb = self.scr_b16(RK + 4096, [T])
        qgT = self.scr_b16(RK + 8192, [T])
        u = self.scr_f32(RU, [NTL, 128])
        wT = self.scr_b16(RU + 8192, [T])
        tmpr = lambda k: self.slots[:, k * TT:(k + 1) * TT]
        tmpR = tmpr
        S.label = 'dn_gates'
        wab = self.scr_b16(RW, [KC, 16])
        S.dma("pool", wab, self.d["dnwab"][m].rearrange("p (k n) -> p k n", k=KC))
        a_sb = self.scr_f32(RC, [T])
        b_sb = self.scr_f32(RC + 8192, [T])
        gc = self.scr_f32(RK, [T])
        cmk = self.scr_f32(RK + 8192, [T])
        S.dma("sp", cmk[0:8, :], self.d["cmask"][:, :])
        S.act(self.nexpa[:], self.dnhp[:, 2 * m:2 * m + 1], AF.Exp)
        S.ts(self.nexpa[:], self.nexpa[:], -1.0, ALU.mult)
        for tt in range(NT):
            sl = slice(tt * TT, (tt + 1) * TT)
            pa = ps[0]
            pb = ps[1]
            for kc in range(KC):
                S.mm(pa[0:8, :], wab[:, kc, 0:8], self.hb[:, kc, sl], start=(kc == 0), stop=(kc == KC - 1), lazy=True)
            for kc in range(KC):
                S.mm(pb[0:8, :], wab[:, kc, 8:16], self.hb[:, kc, sl], start=(kc == 0), stop=(kc == KC - 1), lazy=True)
            S.act(a_sb[0:8, sl], pa[0:8, :], AF.Exp, bias=self.dnhp[:, 2 * m + 1:2 * m + 2])
            S.act(a_sb[0:8, sl], a_sb[0:8, sl], AF.Ln, bias=1.0)
            S.act(b_sb[0:8, sl], pb[0:8, :], AF.Sigmoid)
        S.ts(a_sb[0:8, :], a_sb[0:8, :], self.nexpa[:, 0:1], ALU.mult)
        S.op("dve", lambda e: e.tensor_tensor_scan(out=gc[0:8, :], data0=cmk[0:8, :], data1=a_sb[0:8, :],
                                                   initial=0.0, op0=ALU.mult, op1=ALU.add),
             reads=[cmk[0:8, :], a_sb[0:8, :]], writes=[gc[0:8, :]])
        for n in range(NTL):
            S.transpose(ps[2][:, n * 8:(n + 1) * 8], gc[0:8, n * 128:(n + 1) * 128], self.identf[0:8, 0:8])
            S.transpose(ps[3][:, n * 8:(n + 1) * 8], b_sb[0:8, n * 128:(n + 1) * 128], self.identf[0:8, 0:8])
        S.copy(self.gtok[:], ps[2][:, 0:128])
        S.copy(self.btok[:], ps[3][:, 0:128], eng="act")
        S.mm(ps[4][:, 0:128], self.sel127[:], self.gtok[:])
        S.tt(self.ekd[:], ps[4][:, 0:128], self.gtok[:], ALU.subtract)
        S.act(self.ekd[:], self.ekd[:], AF.Exp)
        S.act(self.bkg[:], self.gtok[:], AF.Exp)
        S.tt(self.bkg[:], self.bkg[:], self.btok[:], ALU.mult)
        mUi = self.maskc[:, 0:128].unsqueeze(1).to_broadcast([128, 4, 128])
        mLs = self.maskc[:, 256:384].unsqueeze(1).to_broadcast([128, 4, 128])
        idb4 = self.identf[:, :].unsqueeze(1).to_broadcast([128, 4, 128])
        v4 = lambda ap: ap.rearrange("p (a b) -> p a b", a=4)
        nbank = [0]

        def project(wsrc, c0, evac):
            for tt in range(NT):
                sl = slice(tt * TT, (tt + 1) * TT)
                pp = ps[nbank[0] % 4]
                nbank[0] += 1
                for kc in range(KC):
                    S.mm(pp[:], wsrc[:, kc, c0:c0 + 128], self.hb[:, kc, sl], start=(kc == 0), stop=(kc == KC - 1), lazy=True)
                evac(tt, sl, pp)

        def conv_silu(c24):
            for k in range(4):
                S.ts(dgs[:, k, :], self.identb[:], sconv(c24, k), ALU.mult)
            for tt in range(NT):
                sl = slice(tt * TT, (tt + 1) * TT)
                pc = ps[4 + tt % 2]
                for k in range(4):
                    S.mm(pc[:], dgs[:, k, :], cin[:, tt * TT + k:tt * TT + k + TT], start=(k == 0), stop=(k == 3), lazy=True)
                S.act(cf[:, sl], pc[:], AF.Silu)

        def to_cin(tt, sl, pp):
            S.copy(cin[:, 3 + tt * TT:3 + (tt + 1) * TT], pp[:], eng=("act" if tt % 2 == 0 else "dve"))

        def l2norm(post):
            for tt in range(NT):
                sl = slice(tt * TT, (tt + 1) * TT)
                sq = self.sq[tt % 2]
                pst = ps[6 + tt % 2]
                S.act(sq[:], cf[:, sl], AF.Square)
                S.mm(pst[:], self.onesb[:], sq[:])
                r = self.rstd[tt % 2]
                S.act(r[:], pst[:], AF.Ln, bias=EPS)
                S.act(r[:], r[:], AF.Exp, scale=-0.5)
                post(sl, r)

        for hd in range(8):
            col = lambda n: n * 8 + hd
            S.dma("pool", wqk, self.d["dnwin"][m, hd, 0].rearrange("p (k n) -> p k n", k=KC))
            S.dma("pool", wvz, self.d["dnwin"][m, hd, 1].rearrange("p (k n) -> p k n", k=KC))
            S.memset(cin[:, 0:3], 0.0)
            S.label = f'h{hd}_z'
            project(wvz, 128, lambda tt, sl, pp: S.act(gz[:, sl], pp[:], AF.Silu))
            S.label = f'h{hd}_v'
            project(wvz, 0, to_cin)
            conv_silu(16 + hd)
            for bt in range(4):
                pv_ = ps[2 + bt % 2]
                for t in range(4):
                    n = bt * 4 + t
                    S.transpose(pv_[:, t * 128:(t + 1) * 128], cf[:, n * 128:(n + 1) * 128], self.identf[:])
                for t in range(4):
                    n = bt * 4 + t
                    if t % 2 == 0:
                        S.ts(vb[:, n, :], pv_[:, t * 128:(t + 1) * 128], self.btok[:, col(n):col(n) + 1], ALU.mult)
                    else:
                        S.act(vb[:, n, :], pv_[:, t * 128:(t + 1) * 128], AF.Identity, scale=self.btok[:, col(n):col(n) + 1])
            S.label = f'h{hd}_k'
            project(wqk, 128, to_cin)
            conv_silu(8 + hd)

            def post_k(sl, r):
                S.tt(cf[:, sl], cf[:, sl], r[:], ALU.mult)
                S.copy(kTb[:, sl], cf[:, sl], eng="act")
            l2norm(post_k)
            for bt in range(4):
                pk = ps[bt % 2]
                for t in range(4):
                    n = bt * 4 + t
                    S.transpose(pk[:, t * 128:(t + 1) * 128], cf[:, n * 128:(n + 1) * 128], self.identf[:])
                for t in range(4):
                    n = bt * 4 + t
                    S.ts(kbg[:, n, :], pk[:, t * 128:(t + 1) * 128], self.bkg[:, col(n):col(n) + 1], ALU.mult)
                    S.act(kd[:, n, :], pk[:, t * 128:(t + 1) * 128], AF.Identity, scale=self.ekd[:, col(n):col(n) + 1])
            S.label = f'h{hd}_q'
            project(wqk, 0, to_cin)
            conv_silu(hd)

            def post_q(sl, r):
                S.stt(cf[:, sl], cf[:, sl], float(128 ** -0.5), r[:], ALU.mult, ALU.mult)
                S.copy(qTb[:, sl], cf[:, sl], eng="act")
            l2norm(post_q)
            for bt in range(4):
                bsl = slice(bt * TT, (bt + 1) * TT)
                dG = tmpf(4)
                EG = tmpf(5)
                for t in range(4):
                    n = bt * 4 + t
                    S.ts(dG[:, t * 128:(t + 1) * 128], self.onesf[:], self.gtok[:, col(n):col(n) + 1], ALU.mult)
                for t in range(4):
                    tsl = slice(t * 128, (t + 1) * 128)
                    S.transpose(ps[0][:, tsl], dG[:, tsl], self.identf[:])
                S.act(EG, ps[0][:], AF.Exp)
                S.copy(self.egl[:, bt * 4:(bt + 1) * 4], EG[:, 127::128])
                S.tt(qgT[:, bsl], cf[:, bsl], EG, ALU.mult)
            wo_h = self.scr_b16(RC, [D])
            S.dma("pool", wo_h, self.d["dnwout"][m, hd])
            oT = [self.ot0[:], self.scr_f32(RC + 2048, [TT])]
            on = self.scr_b16(R2 + 4 * 2048, [T])
            S.memset(self.Sst[:], 0.0)
            S.memset(self.Sb[0][:], 0.0)

            def s4_gen(bt):
                bsl = slice(bt * TT, (bt + 1) * TT)
                dG = tmpf(0)
                for t in range(4):
                    n = bt * 4 + t
                    S.ts(dG[:, t * 128:(t + 1) * 128], self.onesf[:], self.gtok[:, col(n):col(n) + 1], ALU.mult)
                for t in range(4):
                    tsl = slice(t * 128, (t + 1) * 128)
                    S.transpose(ps[0][:, tsl], dG[:, tsl], self.identf[:])
                for t in range(4):
                    n = bt * 4 + t
                    tsl = slice(t * 128, (t + 1) * 128)
                    nsl = slice(n * 128, (n + 1) * 128)
                    S.mm(ps[1][:, tsl], kTb[:, nsl], kTb[:, nsl])
                yield
                EU = tmpf(1)
                EL = tmpf(2)
                for t in range(4):
                    n = bt * 4 + t
                    tsl = slice(t * 128, (t + 1) * 128)
                    gcol = self.gtok[:, col(n):col(n) + 1]
                    S.ts(EU[:, tsl], ps[0][:, tsl], gcol, ALU.subtract, 0.0, ALU.min)
                    S.ts(EL[:, tsl], ps[0][:, tsl], gcol, ALU.subtract, 0.0, ALU.max)
                S.act(EU, EU, AF.Exp)
                S.act(EL, EL, AF.Exp, scale=-1.0)
                for t in range(4):
                    n = bt * 4 + t
                    tsl = slice(t * 128, (t + 1) * 128)
                    nsl = slice(n * 128, (n + 1) * 128)
                    S.mm(ps[0][:, tsl], kTb[:, nsl], qTb[:, nsl])
                yield
                S.tt(v4(EL), v4(EL), mLs, ALU.mult)
                S.tt(v4(EU), v4(EU), mUi, ALU.mult)
                Af = tmpf(3)
                for t in range(4):
                    n = bt * 4 + t
                    tsl = slice(t * 128, (t + 1) * 128)
                    S.stt(Af[:, tsl], ps[1][:, tsl], self.btok[:, col(n):col(n) + 1], EL[:, tsl], ALU.mult, ALU.mult)
                for t in range(4):
                    tsl = slice(t * 128, (t + 1) * 128)
                    S.transpose(ps[1][:, tsl], Af[:, tsl], self.identf[:])
                S.tt(intraT[:, bsl], ps[0][:], EU, ALU.mult)
                yield
                slotA = [tmpR(0), tmpR(1)]
                slotU = [tmpR(2), tmpR(3)]
                P = tmpR(4)
                Pf = tmpr(4)
                S.copy(slotA[0], Af, eng="dve")
                S.copy(slotU[0], ps[1][:], eng="dve")
                S.tt(v4(P), idb4, v4(ps[1][:]), ALU.subtract)
                pA, pU, pP = ps[2], ps[3], ps[4]

                def square(l):
                    A_c, U_c = slotA[l % 2], slotU[l % 2]
                    for t in range(4):
                        tsl = slice(t * 128, (t + 1) * 128)
                        S.mm(pA[:, tsl], U_c[:, tsl], A_c[:, tsl])
                    if l + 1 < 6:
                        for t in range(4):
                            tsl = slice(t * 128, (t + 1) * 128)
                            S.mm(pU[:, tsl], A_c[:, tsl], U_c[:, tsl])

                def evac(l):
                    S.copy(slotA[(l + 1) % 2], pA[:], eng="act")
                    if l + 1 < 6:
                        S.copy(slotU[(l + 1) % 2], pU[:], eng="dve")
                square(0)
                evac(0)
                yield
                for l in range(1, 7):
                    A_l = slotA[l % 2]
                    if l < 6:
                        square(l)
                    for t in range(4):
                        tsl = slice(t * 128, (t + 1) * 128)
                        S.mm(pP[:, tsl], A_l[:, tsl], P[:, tsl])
                    if l < 6:
                        evac(l)
                        S.tt(P, pP[:], Pf, ALU.add)
                    else:
                        S.tt(Rb[:, bsl], pP[:], Pf, ALU.add)
                    yield
                for t in range(4):
                    n = bt * 4 + t
                    tsl = slice(t * 128, (t + 1) * 128)
                    nsl = slice(n * 128, (n + 1) * 128)
                    S.mm(ps[0][:, tsl], Rb[:, nsl], vb[:, n, :])
                    S.mm(ps[1][:, tsl], kbg[:, n, :], Rb[:, nsl])
                S.copy(u[:, bt * 4:(bt + 1) * 4, :].rearrange("p a b -> p (a b)"), ps[0][:], eng="act")
                S.copy(wT[:, bsl], ps[1][:])
                yield

            def scan_gen(bt):
                bsl = slice(bt * TT, (bt + 1) * TT)
                po = ps[6]
                for t in range(4):
                    n = bt * 4 + t
                    nsl = slice(n * 128, (n + 1) * 128)
                    tsl = slice(t * 128, (t + 1) * 128)
                    Sb = self.Sb[n % 2]
                    Sbn = self.Sb[(n + 1) % 2]
                    vn = self.vnew[n % 2]
                    S.mm(ps[5][:, 0:128], wT[:, nsl], Sb[:])
                    S.mm(po[:, tsl], Sb[:], qgT[:, nsl], start=True, stop=False)
                    yield
                    S.tt(vn[:], u[:, n, :], ps[5][:, 0:128], ALU.subtract)
                    S.mm(po[:, tsl], vn[:], intraT[:, nsl], start=False, stop=True)
                    S.mm(ps[5][:, 128:256], kd[:, n, :], vn[:])
                    S.stt(Sbn[:], self.Sst[:], self.egl[:, n:n + 1], ps[5][:, 128:256], ALU.mult, ALU.add)
                    S.stt(self.Sst[:], self.Sst[:], self.egl[:, n:n + 1], ps[5][:, 128:256], ALU.mult, ALU.add)
                    yield
                sq = self.sq[bt % 2]
                pst = ps[7]
                S.act(sq[:], po[:], AF.Square)
                S.mm(pst[:], self.onesb[:], sq[:])
                r = self.rstd[bt % 2]
                S.act(r[:], pst[:], AF.Ln, scale=1.0 / 128, bias=EPS)
                S.act(r[:], r[:], AF.Exp, scale=-0.5)
                ot = oT[bt % 2]
                S.tt(ot, po[:], r[:], ALU.mult)
                S.stt(on[:, bsl], ot, og, gz[:, bsl], ALU.mult, ALU.mult)
                yield

            def s6_gen(tt):
                sl = slice(tt * TT, (tt + 1) * TT)
                for oc in range(KC):
                    pp = ps[7]
                    S.mm(pp[:], wo_h[:, oc * 128:(oc + 1) * 128], on[:, sl])
                    S.stt(self.xs[:, oc, sl], pp[:], gate[:, oc:oc + 1], self.xs[:, oc, sl], ALU.mult, ALU.add)
                    yield

            for k in range(6):
                S.label = f'h{hd}_pipe{k}'
                gens = []
                if k < 4:
                    gens.append(s4_gen(k))
                if 1 <= k <= 4:
                    gens.append(scan_gen(k - 1))
                if 2 <= k <= 5:
                    gens.append(s6_gen(k - 2))
                while gens:
                    for g in list(gens):
                        try:
                            next(g)
                        except StopIteration:
                            gens.remove(g)

    def _finish(self, s):
        S = self.S
        toks = []
        if self.do_final:
            self._norm(lambda kc: self.fg[:, kc:kc + 1], lambda kc: 0.0,
                       lambda kc, tt: self.xs[:, kc, tt * TT:(tt + 1) * TT])
        for kc in range(KC):
            toks.append(S.dma("sp", self.out[s, kc * 128:(kc + 1) * 128, :], self.xs[:, kc, :]))
        return toks


def _fm(v):
    v = np.asarray(v, np.float32)
    lead = v.shape[:-1]
    return np.ascontiguousarray(np.moveaxis(v.reshape(lead + (KC, 128)), -1, 0))


def prep_shared(inp, layers):
    sh = {}
    L = list(layers)
    sh["ng"] = _fm(inp["norm_g"]).reshape(128, DEPTH * 3 * KC)
    wa = np.asarray(inp["w_ada"], np.float32)
    wa = wa.reshape(DEPTH, KC, 128, 72, 128).transpose(0, 3, 2, 1, 4)
    sh["wada"] = np.ascontiguousarray(wa[L]).reshape(len(L), 72, 128, KC * 128)
    ba = np.asarray(inp["b_ada"], np.float32).reshape(DEPTH, 72, 128)
    sh["bada"] = np.ascontiguousarray(ba.transpose(2, 0, 1)).reshape(128, DEPTH * 72)
    wi = np.asarray(inp["w_ffn_in"], np.float32)
    wi = wi.reshape(DEPTH, 2, KC, 128, 2, FC, 128)
    wi = wi.transpose(0, 1, 5, 3, 2, 4, 6)
    sh["win"] = np.ascontiguousarray(wi[L]).reshape(len(L), 2, FC, 128, KC * 256)
    wo = np.asarray(inp["w_ffn_out"], np.float32)
    wo = wo.reshape(DEPTH, 2, 2, 11, 128, KC, 128)
    wo = wo.transpose(0, 1, 2, 5, 4, 3, 6)
    sh["wout"] = np.ascontiguousarray(wo[L]).reshape(len(L), 2, 2, KC, 128, 11 * 128)
    sh["fg"] = _fm(inp["final_g"]).reshape(128, KC)
    wg = np.asarray(inp["cm_w_glu"], np.float32).reshape(2, KC, 128, 2, KC, 128)
    sh["cmglu"] = np.ascontiguousarray(wg.transpose(0, 4, 2, 1, 3, 5)).reshape(2, KC, 128, KC * 256)
    wp = np.asarray(inp["cm_w_pw"], np.float32).reshape(2, KC, 128, KC, 128)
    sh["cmpw"] = np.ascontiguousarray(wp.transpose(0, 3, 2, 1, 4)).reshape(2, KC, 128, KC * 128)
    sm = np.zeros((128, 2, 296), np.float32)
    bg = _fm(np.asarray(inp["cm_b_glu"], np.float32).reshape(2, 2, D))
    sm[:, :, 0:8] = bg[:, :, 0, :]
    sm[:, :, 8:16] = bg[:, :, 1, :]
    wd = _fm(inp["cm_w_dw"])
    sm[:, :, 16:264] = wd.transpose(0, 1, 3, 2).reshape(128, 2, 248)
    sm[:, :, 264:272] = _fm(inp["cm_b_dw"])
    sm[:, :, 272:280] = _fm(inp["cm_ln_g"])
    sm[:, :, 280:288] = _fm(inp["cm_ln_b"])
    sm[:, :, 288:296] = _fm(inp["cm_b_pw"])
    sh["cmsm"] = sm.reshape(128, 2 * 296)
    W = 1024
    dw = np.asarray(inp["dn_w_in"], np.float32)
    parts = [dw[:, :, c * W:(c + 1) * W].reshape(2, KC, 128, 8, 128) for c in range(4)]
    qk = np.stack([parts[0], parts[1]], axis=4)
    vz = np.stack([parts[2], parts[3]], axis=4)
    both = np.stack([qk, vz], axis=4)
    sh["dnwin"] = np.ascontiguousarray(both.transpose(0, 3, 4, 2, 1, 5, 6)).reshape(2, 8, 2, 128, KC * 256)
    ab = dw[:, :, 4 * W:4 * W + 16].reshape(2, KC, 128, 16)
    sh["dnwab"] = np.ascontiguousarray(ab.transpose(0, 2, 1, 3)).reshape(2, 128, KC * 16)
    sh["dnwout"] = np.ascontiguousarray(np.asarray(inp["dn_w_out"], np.float32).reshape(2, 8, 128, D))
    dsm = np.zeros((128, 2, 97), np.float32)
    sc = np.asarray(inp["dn_w_sconv"], np.float32).reshape(2, 4, 24, 128)
    dsm[:, :, 0:96] = sc.transpose(3, 0, 2, 1).reshape(128, 2, 96)
    dsm[:, :, 96] = np.asarray(inp["dn_o_g"], np.float32).T
    sh["dnsm"] = dsm.reshape(128, 2 * 97)
    hp = np.zeros((8, 4), np.float32)
    for mm_ in range(2):
        hp[:, 2 * mm_] = np.asarray(inp["dn_a_log"], np.float32)[mm_]
        hp[:, 2 * mm_ + 1] = np.asarray(inp["dn_dt_bias"], np.float32)[mm_]
    sh["dnhp"] = hp
    pi = np.arange(128)[:, None]
    fi = np.arange(128)[None, :]
    sh["maskc"] = np.concatenate([(fi >= pi), (fi > pi), (pi > fi)], axis=1).astype(np.float32)
    s127 = np.zeros((128, 128), np.float32)
    s127[127, :] = 1.0
    sh["sel127"] = s127
    cm = np.ones((8, T), np.float32)
    cm[:, ::128] = 0.0
    sh["cmask"] = cm
    sh["identf"] = np.eye(128, dtype=np.float32)
    return sh


def prep_core(inp, c):
    m = {}
    x = np.asarray(inp["x"], np.float32)
    m["xT"] = np.ascontiguousarray(x[2 * c:2 * c + 2].transpose(0, 2, 1))
    cc = np.asarray(inp["c"], np.float32)[2 * c:2 * c + 2]
    m["cT"] = np.ascontiguousarray(cc.reshape(2, KC, 128).transpose(2, 1, 0)).reshape(128, KC * 2)
    return m


_PROG_CACHE = {}


def run(inputs, stages=None, nseq=2, do_final=True, ncores=NCORES, debug=False):
    key = (None if stages is None else tuple(stages), nseq, do_final, debug)
    if key not in _PROG_CACHE:
        _PROG_CACHE[key] = Prog(stages, nseq, do_final, debug)
    prog = _PROG_CACHE[key]
    sh = prep_shared(inputs, prog.layers)
    in_maps = []
    for c in range(ncores):
        m = dict(sh)
        m.update(prep_core(inputs, c))
        in_maps.append(m)
    res = run_bass_kernel_spmd(prog.nc, in_maps, core_ids=list(range(ncores)))
    if debug:
        global LAST_DBG
        LAST_DBG = np.asarray(res.results[0]["dbg"])
    B = 2 * ncores
    out = np.empty((B, T, D), np.float32)
    for c in range(ncores):
        o = np.asarray(res.results[c]["outT"])
        out[2 * c:2 * c + 2] = o.transpose(0, 2, 1)
    return out


def kernel(**inputs):
    return run(inputs)
```

```python
import contextlib
import numpy as np
import concourse.bass as bass
import concourse.mybir as mybir
from concourse.bass_utils import run_bass_kernel_spmd

F32 = mybir.dt.float32
BF16 = mybir.dt.bfloat16
AF = mybir.ActivationFunctionType
ALU = mybir.AluOpType

D = 1024
T = 2048
KC = 8
TT = 512
NT = T // TT
FF = 2816
FC = 22
DEPTH = 4
EPS = 1e-6
NCORES = 8
ENGS = ("pe", "act", "dve", "pool", "sp")
F32R = mybir.dt.float32r
_ESZ = {F32: 4, BF16: 2, F32R: 4}


def _res(ap):
    a = ap.ap
    ps = a[0][0]
    off = ap.offset % ps if ps else ap.offset
    span = 1
    for s, n in a[1:]:
        span += (n - 1) * abs(s)
    e = _ESZ[ap.dtype]
    if "PSUM" in str(ap.space).upper():
        return ("@" + ap.tensor.name, 0, 1 << 20)
    return (ap.tensor.name, off * e, (off + span) * e)


class Sched:
    NDMA = 6

    def __init__(self, nc, stack):
        self.nc = nc
        self.sems = {}
        for e in ENGS:
            self.sems[("p", e)] = stack.enter_context(nc.semaphore(f"prog_{e}"))
        for q in ("sp", "pool"):
            for i in range(self.NDMA):
                self.sems[("d", q, i)] = stack.enter_context(nc.semaphore(f"dma_{q}_{i}"))
        self.cnt = {k: 0 for k in self.sems}
        self.seen = {e: {} for e in ENGS}
        self.prog = {e: [] for e in ENGS}
        self.ents = {}
        self.dma_i = {"sp": 0, "pool": 0}
        self.dma_last = {}
        self.nops = {e: 0 for e in ENGS}
        self.label = ''
        self.labels = {e: [] for e in ENGS}

    def _collect(self, e, reads, writes, extra=()):
        need = {}

        def add(tok):
            if tok is not None and need.get(tok[0], 0) < tok[1]:
                need[tok[0]] = tok[1]
        own = ("p", e)
        for (name, lo, hi) in reads:
            for ent in self.ents.get(name, ()):
                if ent[0] < hi and lo < ent[1]:
                    add(ent[2])
                    if name[0] == "@":
                        for k, v in ent[3].items():
                            if k != own:
                                add((k, v))
        for (name, lo, hi) in writes:
            for ent in self.ents.get(name, ()):
                if ent[0] < hi and lo < ent[1]:
                    add(ent[2])
                    for k, v in ent[3].items():
                        add((k, v))
        for t in extra:
            add(t)
        waits = []
        seen = self.seen[e]
        for k, v in need.items():
            if e == "pe" and k == ("p", "pe"):
                continue
            if seen.get(k, 0) >= v:
                continue
            seen[k] = v
            waits.append((k, v))
        return waits

    def _commit(self, reads, writes, tok):
        k, v = tok
        for (name, lo, hi) in reads:
            ents = self.ents.setdefault(name, [])
            covered = False
            for ent in ents:
                if ent[0] < hi and lo < ent[1]:
                    if ent[3].get(k, 0) < v:
                        ent[3][k] = v
                    if ent[0] <= lo and hi <= ent[1]:
                        covered = True
            if not covered:
                ents.append([lo, hi, None, {k: v}])
        for (name, lo, hi) in writes:
            ents = self.ents.setdefault(name, [])
            new = [ent for ent in ents
                   if not (ent[0] < hi and lo < ent[1] and lo <= ent[0] and ent[1] <= hi)]
            new.append([lo, hi, tok, {}])
            self.ents[name] = new

    def op(self, e, fn, reads=(), writes=(), signal=True):
        reads = [_res(a) for a in reads]
        writes = [_res(a) for a in writes]
        waits = self._collect(e, reads, writes)
        pk = ("p", e)
        if signal:
            self.cnt[pk] += 1
            tok = (pk, self.cnt[pk])
        else:
            tok = (pk, self.cnt[pk] + 1)
        self._commit(reads, writes, tok)
        sems = self.sems
        self.nops[e] += 1
        self.labels[e].append(self.label)

        def emit(engine, waits=waits, fn=fn, signal=signal, pk=pk):
            for k, v in waits:
                engine.wait_ge(sems[k], v)
            ins = fn(engine)
            if signal:
                ins.then_inc(sems[pk], 1)
        self.prog[e].append(emit)
        return tok

    def dma(self, q, out, in_, **kw):
        reads = [_res(in_)] if _is_onchip(in_) else []
        writes = [_res(out)] if _is_onchip(out) else []
        i = self.dma_i[q] % self.NDMA
        self.dma_i[q] += 1
        dk = ("d", q, i)
        extra = []
        if dk in self.dma_last:
            extra.append(self.dma_last[dk])
        waits = self._collect(q, reads, writes, extra)
        self.cnt[dk] += 16
        tok = (dk, self.cnt[dk])
        self.dma_last[dk] = tok
        self._commit(reads, writes, tok)
        sems = self.sems
        self.nops[q] += 1

        def emit(engine, waits=waits, out=out, in_=in_, dk=dk, kw=kw):
            for k, v in waits:
                engine.wait_ge(sems[k], v)
            engine.dma_start(out=out, in_=in_, **kw).then_inc(sems[dk], 16)
        self.prog[q].append(emit)
        return tok

    def wait_tokens(self, e, toks):
        sems = self.sems
        ws = []
        for k, v in toks:
            if self.seen[e].get(k, 0) >= v:
                continue
            self.seen[e][k] = v
            ws.append((k, v))

        def emit(engine, ws=ws):
            for k, v in ws:
                engine.wait_ge(sems[k], v)
        self.prog[e].append(emit)

    def replay(self, block):
        prog = self.prog

        @block.sync
        def _(eng):
            for f in prog["sp"]:
                f(eng)

        @block.scalar
        def _(eng):
            for f in prog["act"]:
                f(eng)

        @block.vector
        def _(eng):
            for f in prog["dve"]:
                f(eng)

        @block.gpsimd
        def _(eng):
            for f in prog["pool"]:
                f(eng)

        @block.tensor
        def _(eng):
            for f in prog["pe"]:
                f(eng)

    def mm(self, out, lhsT, rhs, start=True, stop=True, lazy=False):
        self.op("pe", lambda e: e.matmul(out, lhsT=lhsT, rhs=rhs, start=start, stop=stop),
                reads=[lhsT, rhs], writes=[out], signal=(stop or not lazy))

    def transpose(self, out, in_, ident):
        self.op("pe", lambda e: e.transpose(out, in_, ident), reads=[in_, ident], writes=[out])

    def act(self, out, in_, func, scale=1.0, bias=0.0):
        rd = [in_]
        if not isinstance(scale, (int, float)):
            rd.append(scale)
        if not isinstance(bias, (int, float)):
            rd.append(bias)
        self.op("act", lambda e: e.activation(out=out, in_=in_, func=func, scale=scale, bias=bias),
                reads=rd, writes=[out])

    def tt(self, out, in0, in1, op, eng="dve"):
        self.op(eng, lambda e: e.tensor_tensor(out=out, in0=in0, in1=in1, op=op),
                reads=[in0, in1], writes=[out])

    def ts(self, out, in0, s1, op0, s2=None, op1=None, eng="dve"):
        rd = [in0]
        if not isinstance(s1, (int, float)):
            rd.append(s1)
        if s2 is not None and not isinstance(s2, (int, float)):
            rd.append(s2)
        if op1 is None:
            self.op(eng, lambda e: e.tensor_scalar(out=out, in0=in0, scalar1=s1, scalar2=None, op0=op0),
                    reads=rd, writes=[out])
        else:
            self.op(eng, lambda e: e.tensor_scalar(out=out, in0=in0, scalar1=s1, scalar2=s2, op0=op0, op1=op1),
                    reads=rd, writes=[out])

    def stt(self, out, in0, scalar, in1, op0, op1):
        rd = [in0, in1]
        if not isinstance(scalar, (int, float)):
            rd.append(scalar)
        self.op("dve", lambda e: e.scalar_tensor_tensor(out=out, in0=in0, scalar=scalar, in1=in1, op0=op0, op1=op1),
                reads=rd, writes=[out])

    def copy(self, out, in_, eng="dve"):
        if eng == "act":
            self.op("act", lambda e: e.copy(out=out, in_=in_), reads=[in_], writes=[out])
        else:
            self.op(eng, lambda e: e.tensor_copy(out=out, in_=in_), reads=[in_], writes=[out])

    def memset(self, ap, val, eng="dve"):
        self.op(eng, lambda e: e.memset(ap, val), reads=[], writes=[ap])


def _is_onchip(ap):
    return "DRAM" not in str(ap.space).upper()


class Prog:
    def __init__(self, stages=None, nseq=2, do_final=True, debug=False):
        self.debug = debug
        if stages is None:
            stages = [(i, j) for i in range(DEPTH) for j in range(3)]
        self.stages = list(stages)
        self.layers = sorted(set(i for i, _ in self.stages))
        self.nseq = nseq
        self.do_final = do_final
        self.nc = bass.Bass("TRN2", target_bir_lowering=False)
        self._build()

    def _dram(self):
        nc = self.nc

        def din(name, shape):
            return nc.dram_tensor(name, list(shape), F32, kind="ExternalInput").ap()
        d = {}
        d["xT"] = din("xT", [2, D, T])
        d["cT"] = din("cT", [128, KC * 2])
        NL = len(self.layers)
        d["ng"] = din("ng", [128, DEPTH * 3 * KC])
        d["wada"] = din("wada", [NL, 72, 128, KC * 128])
        d["bada"] = din("bada", [128, DEPTH * 72])
        d["win"] = din("win", [NL, 2, FC, 128, KC * 256])
        d["wout"] = din("wout", [NL, 2, 2, KC, 128, 11 * 128])
        d["fg"] = din("fg", [128, KC])
        d["cmglu"] = din("cmglu", [2, KC, 128, KC * 256])
        d["cmpw"] = din("cmpw", [2, KC, 128, KC * 128])
        d["cmsm"] = din("cmsm", [128, 2 * 296])
        d["dnwin"] = din("dnwin", [2, 8, 2, 128, KC * 256])
        d["dnwab"] = din("dnwab", [2, 128, KC * 16])
        d["dnwout"] = din("dnwout", [2, 8, 128, D])
        d["dnsm"] = din("dnsm", [128, 2 * 97])
        d["dnhp"] = din("dnhp", [8, 4])
        d["maskc"] = din("maskc", [128, 3 * 128])
        d["sel127"] = din("sel127", [128, 128])
        d["cmask"] = din("cmask", [8, T])
        d["identf"] = din("identf", [128, 128])
        self.out = nc.dram_tensor("outT", [2, D, T], F32, kind="ExternalOutput").ap()
        self.dbg = nc.dram_tensor("dbg", [128, 8192], F32, kind="ExternalOutput").ap() if self.debug else None
        self.d = d

    def _build(self):
        nc = self.nc
        self._dram()
        with contextlib.ExitStack() as st:
            def sb(name, shape, dt):
                return st.enter_context(nc.sbuf_tensor(name, list(shape), dt))
            self.xs = sb("xs", [128, KC, T], F32)
            self.hb = sb("hb", [128, KC, T], BF16)
            self.SCR_BYTES = 65 * 1024
            self.scr = sb("scr", [128, self.SCR_BYTES // 4], F32)
            self.scr_bf = self.scr[:, :].bitcast(BF16)
            self.slots = sb("slots", [128, 5 * TT], F32)
            self.ot0 = sb("ot0", [128, TT], F32)
            self.mod = sb("mod", [128, DEPTH, 2, 72], F32)
            self.msc = sb("msc", [128, DEPTH, 2, 3, 3, KC], F32)
            self.ct = sb("ct", [128, KC * 2], F32)
            self.cs = sb("cs", [128, KC * 2], F32)
            self.ng = sb("ngs", [128, DEPTH * 3 * KC], F32)
            self.bada = sb("badas", [128, DEPTH * 72], F32)
            self.fg = sb("fgs", [128, KC], F32)
            self.identf = sb("identfs", [128, 128], F32)
            self.onesb = sb("onesb", [128, 128], BF16)
            self.identb = sb("identb", [128, 128], BF16)
            self.cmsm = sb("cmsms", [128, 2 * 296], F32)
            self.cbg = sb("cbg", [128, KC], F32)
            self.dnsm = sb("dnsms", [128, 2 * 97], F32)
            self.dnhp = sb("dnhps", [8, 4], F32)
            self.maskc = sb("maskcs", [128, 3 * 128], F32)
            self.sel127 = sb("sel127s", [128, 128], F32)
            self.onesf = sb("onesf", [128, 128], F32)
            self.nonesf = sb("nonesf", [128, 128], F32)
            self.nexpa = sb("nexpa", [8, 1], F32)
            self.gtok = sb("gtok", [128, 128], F32)
            self.btok = sb("btok", [128, 128], F32)
            self.ekd = sb("ekd", [128, 128], F32)
            self.bkg = sb("bkg", [128, 128], F32)
            self.egl = sb("egl", [128, 16], F32)
            self.Sst = sb("Sst", [128, 128], F32)
            self.Sb = [sb(f"Sb{i}", [128, 128], BF16) for i in range(2)]
            self.vnew = [sb(f"vnew{i}", [128, 128], BF16) for i in range(2)]
            self.sqb = [sb(f"sqb{i}", [128, TT], BF16) for i in range(2)]
            self.sq = [sb(f"sq{i}", [128, TT], BF16) for i in range(2)]
            self.rstd = [sb(f"rstd{i}", [128, TT], F32) for i in range(2)]
            self.xn = [sb(f"xn{i}", [128, TT], F32) for i in range(2)]
            self.sg = [sb(f"sg{i}", [128, TT], F32) for i in range(2)]
            self.ps = [st.enter_context(nc.psum_tensor(f"ps{i}", [128, TT], F32)) for i in range(8)]
            self.S = Sched(nc, st)
            block = st.enter_context(nc.Block())
            self._emit_all()
            self.S.replay(block)

    def scr_f32(self, byte_off, shape):
        n = int(np.prod(shape))
        v = self.scr[:, byte_off // 4: byte_off // 4 + n]
        return self._shape(v, shape)

    def scr_b16(self, byte_off, shape):
        n = int(np.prod(shape))
        v = self.scr_bf[:, byte_off // 2: byte_off // 2 + n]
        return self._shape(v, shape)

    @staticmethod
    def _shape(v, shape):
        if len(shape) == 1:
            return v
        if len(shape) == 2:
            return v.rearrange("p (a b) -> p a b", a=shape[0])
        if len(shape) == 3:
            return v.rearrange("p (a b c) -> p a b c", a=shape[0], b=shape[1])
        raise ValueError

    def _emit_all(self):
        S = self.S
        d = self.d
        S.dma("sp", self.ct[:], d["cT"][:, :])
        S.dma("sp", self.ng[:], d["ng"][:, :])
        S.dma("sp", self.bada[:], d["bada"][:, :])
        S.dma("sp", self.fg[:], d["fg"][:, :])
        S.dma("sp", self.identf[:], d["identf"][:, :])
        S.memset(self.onesb[:], 1.0)
        S.dma("sp", self.cmsm[:], d["cmsm"][:, :])
        S.copy(self.identb[:], self.identf[:])
        S.dma("sp", self.dnsm[:], d["dnsm"][:, :])
        S.dma("sp", self.dnhp[:], d["dnhp"][:, :])
        S.dma("sp", self.maskc[:], d["maskc"][:, :])
        S.dma("sp", self.sel127[:], d["sel127"][:, :])
        S.memset(self.onesf[:], 1.0)
        S.memset(self.nonesf[:], -1.0)
        S.act(self.cs[:], self.ct[:], AF.Silu)
        self._adaln()
        toks = []
        for s in range(self.nseq):
            self._load_x(s)
            for (i, j) in self.stages:
                if j == 0:
                    self._ffn(i, s, 0)
                elif j == 1:
                    self._mixer(i, s)
                elif j == 2:
                    self._ffn(i, s, 1)
                elif j == 8:
                    self._norm_mod(i, s, 0)
            if self.debug and s == 0:
                toks += self._dump()
            toks += self._finish(s)
        S.wait_tokens("sp", toks)

    def _load_x(self, s):
        for kc in range(KC):
            self.S.dma("sp", self.xs[:, kc, :], self.d["xT"][s, kc * 128:(kc + 1) * 128, :])

    def _adaln(self):
        S = self.S
        bufs = [self.scr_f32(0, [4, KC * 128]), self.scr_f32(16384, [4, KC * 128])]
        for i in self.layers:
            ps = self.ps[7]
            psv = ps[:, 0:144].rearrange("p (s c) -> p c s", s=2)
            for g in range(18):
                buf = bufs[g % 2]
                S.dma("sp", buf, self.d["wada"][self.layers.index(i), g * 4:(g + 1) * 4].rearrange("c p n -> p c n"))
                for j in range(4):
                    ch = g * 4 + j
                    for kc in range(KC):
                        S.mm(psv[:, ch, :], buf[:, j, kc * 128:(kc + 1) * 128],
                             self.cs[:, kc * 2:(kc + 1) * 2], start=(kc == 0), stop=(kc == KC - 1))
            for s in range(2):
                S.tt(self.mod[:, i, s, :], ps[:, s * 72:(s + 1) * 72], self.bada[:, i * 72:(i + 1) * 72], ALU.add)
            for s in range(2):
                for j in range(3):
                    m = self.mod[:, i, s, j * 24:(j + 1) * 24]
                    ngv = self.ng[:, (i * 3 + j) * KC:(i * 3 + j + 1) * KC]
                    S.stt(self.msc[:, i, s, j, 0, :], m[:, 8:16], 1.0, ngv, ALU.add, ALU.mult)
                    S.copy(self.msc[:, i, s, j, 1, :], m[:, 0:8])
                    rw = 1.0 if j == 1 else 0.5
                    S.ts(self.msc[:, i, s, j, 2, :], m[:, 16:24], 1.0, ALU.add, rw, ALU.mult)

    def _norm(self, gs_fn, shift_fn, out_fn, tts=range(NT)):
        S = self.S
        for tt in tts:
            sl = slice(tt * TT, (tt + 1) * TT)
            ps = self.ps[6 + (tt % 2)]
            for kc in range(KC):
                if kc % 2 == 0:
                    sq = self.sq[(kc // 2) % 2]
                    S.act(sq[:], self.xs[:, kc, sl], AF.Square)
                else:
                    sq = self.sqb[(kc // 2) % 2]
                    S.tt(sq[:], self.xs[:, kc, sl], self.xs[:, kc, sl], ALU.mult, eng="pool")
                S.mm(ps[:], self.onesb[:], sq[:], start=(kc == 0), stop=(kc == KC - 1))
            r = self.rstd[tt % 2]
            S.act(r[:], ps[:], AF.Ln, scale=1.0 / D, bias=EPS)
            S.act(ps[:], r[:], AF.Exp, scale=-0.5)
            for kc in range(KC):
                xn = self.xn[kc % 2]
                S.tt(xn[:], self.xs[:, kc, sl], ps[:], ALU.mult)
                S.act(out_fn(kc, tt), xn[:], AF.Identity, scale=gs_fn(kc), bias=shift_fn(kc))

    def _norm_mod(self, i, s, j):
        self._norm(lambda kc: self.msc[:, i, s, j, 0, kc:kc + 1],
                   lambda kc: self.msc[:, i, s, j, 1, kc:kc + 1],
                   lambda kc, tt: self.hb[:, kc, tt * TT:(tt + 1) * TT])

    def _ffn(self, i, s, which):
        S = self.S
        j = 0 if which == 0 else 2
        self._norm_mod(i, s, j)
        hid = self.scr_b16(0, [11, T])
        winb = [self.scr_b16(45056 + b * 4096, [KC, 256]) for b in range(3)]
        woutb = [self.scr_b16(45056 + 12288 + b * 2816, [11, 128]) for b in range(2)]
        win_d = self.d["win"]
        wout_d = self.d["wout"]
        gatec = lambda kc: self.msc[:, i, s, j, 2, kc:kc + 1]
        n_in = 0
        n_out = 0
        for half in range(2):
            for f in range(11):
                fc = half * 11 + f
                wb = winb[n_in % 3]
                n_in += 1
                S.dma("pool", wb, win_d[self.layers.index(i), which, fc].rearrange("p (k n) -> p k n", k=KC))
                for tt in range(NT):
                    sl = slice(tt * TT, (tt + 1) * TT)
                    pg = self.ps[(f * NT + tt) % 2]
                    pu = self.ps[2 + (f * NT + tt) % 2]
                    for kc in range(KC):
                        S.mm(pg[:], wb[:, kc, 0:128], self.hb[:, kc, sl], start=(kc == 0), stop=(kc == KC - 1), lazy=True)
                    for kc in range(KC):
                        S.mm(pu[:], wb[:, kc, 128:256], self.hb[:, kc, sl], start=(kc == 0), stop=(kc == KC - 1), lazy=True)
                    sg = self.sg[(f * NT + tt) % 2]
                    S.act(sg[:], pg[:], AF.Silu)
                    S.tt(hid[:, f, sl], pu[:], sg[:], ALU.mult)
            for dc in range(KC):
                wo = woutb[n_out % 2]
                n_out += 1
                S.dma("pool", wo, wout_d[self.layers.index(i), which, half, dc].rearrange("p (k n) -> p k n", k=11))
                for tt in range(NT):
                    sl = slice(tt * TT, (tt + 1) * TT)
                    po = self.ps[4 + (dc * NT + tt) % 2]
                    for f in range(11):
                        S.mm(po[:], wo[:, f, :], hid[:, f, sl], start=(f == 0), stop=(f == 10), lazy=True)
                    S.stt(self.xs[:, dc, sl], po[:], gatec(dc), self.xs[:, dc, sl], ALU.mult, ALU.add)

    def _mixer(self, i, s):
        if i % 2 == 0:
            self._conv_mixer(i, s)
        else:
            self._dn_mixer(i, s)

    def _conv_mixer(self, i, s):
        S = self.S
        a = i // 2
        j = 1
        S.label = 'cv_norm'
        self._norm_mod(i, s, j)
        S.label = 'cv_glu'
        base = a * 296
        sm = self.cmsm
        bga = lambda oc: sm[:, base + oc:base + oc + 1]
        bgb = lambda oc: sm[:, base + 8 + oc:base + 8 + oc + 1]
        wdw = lambda oc, k: sm[:, base + 16 + oc * 31 + k:base + 16 + oc * 31 + k + 1]
        bdw = lambda oc: sm[:, base + 264 + oc:base + 264 + oc + 1]
        lng = lambda oc: sm[:, base + 272 + oc:base + 272 + oc + 1]
        lnb = lambda oc: sm[:, base + 280 + oc:base + 280 + oc + 1]
        gate = self.msc[:, i, s, j, 2, :]
        S.tt(self.cbg[:], sm[:, base + 288:base + 296], gate, ALU.mult)
        UW = 30 + T
        u = self.scr_b16(0, [KC, UW])
        wgl = [self.scr_b16(33280 + b * 4096, [KC, 256]) for b in range(2)]
        wpw = [self.scr_b16(33280 + b * 2048, [KC, 128]) for b in range(2)]
        diag = [self.scr_b16(33280, [31, 128]), self.scr_b16(57856, [31, 128])]
        sgt = [self.sg[0][:], self.sg[1][:]]
        u2 = self.scr_b16(41472, [KC, 1024])
        yv = self.hb[:, :, :].rearrange("p a b -> p (a b)").bitcast(F32)
        y = yv.rearrange("p (a b) -> p a b", a=KC)
        S.memset(u[:, :, 0:30], 0.0)
        for oc in range(KC):
            wb = wgl[oc % 2]
            S.dma("pool", wb, self.d["cmglu"][a, oc].rearrange("p (k n) -> p k n", k=KC))
            for tt in range(NT):
                sl = slice(tt * TT, (tt + 1) * TT)
                pa = self.ps[(oc * NT + tt) % 2]
                pb = self.ps[2 + (oc * NT + tt) % 2]
                for kc in range(KC):
                    S.mm(pa[:], wb[:, kc, 0:128], self.hb[:, kc, sl], start=(kc == 0), stop=(kc == KC - 1), lazy=True)
                for kc in range(KC):
                    S.mm(pb[:], wb[:, kc, 128:256], self.hb[:, kc, sl], start=(kc == 0), stop=(kc == KC - 1), lazy=True)
                sg = sgt[(oc * NT + tt) % 2]
                S.act(sg, pb[:], AF.Sigmoid, bias=bgb(oc))
                S.stt(u[:, oc, 30 + tt * TT:30 + (tt + 1) * TT], pa[:], bga(oc), sg, ALU.add, ALU.mult)
        npw = 0
        for hf in range(2):
            S.label = f'cv_conv{hf}'
            for oc in range(KC):
                dg = diag[oc % 2]
                for k in range(31):
                    S.ts(dg[:, k, :], self.identb[:], wdw(oc, k), ALU.mult)
                for t2 in range(2):
                    tt = hf * 2 + t2
                    pc = self.ps[4 + (oc * 2 + t2) % 2]
                    for k in range(31):
                        S.mm(pc[:], dg[:, k, :], u[:, oc, tt * TT + k:tt * TT + k + TT],
                             start=(k == 0), stop=(k == 30), lazy=True)
                    S.act(y[:, oc, t2 * TT:(t2 + 1) * TT], pc[:], AF.Identity, bias=bdw(oc))
            S.label = f'cv_ln{hf}'
            for t2 in range(2):
                ysl = slice(t2 * TT, (t2 + 1) * TT)
                psA = self.ps[6]
                psB = self.ps[7]
                for kc in range(KC):
                    qa = self.sq[kc % 2]
                    qb = self.sqb[kc % 2]
                    S.copy(qa[:], y[:, kc, ysl], eng="act")
                    S.act(qb[:], y[:, kc, ysl], AF.Square)
                    S.mm(psA[:], self.onesb[:], qa[:], start=(kc == 0), stop=(kc == KC - 1))
                    S.mm(psB[:], self.onesb[:], qb[:], start=(kc == 0), stop=(kc == KC - 1))
                m = self.rstd[0]
                r = self.rstd[1]
                S.act(m[:], psA[:], AF.Identity, scale=1.0 / D)
                S.tt(r[:], m[:], m[:], ALU.mult)
                S.stt(r[:], psB[:], 1.0 / D, r[:], ALU.mult, ALU.subtract)
                S.act(r[:], r[:], AF.Ln, bias=EPS)
                S.act(r[:], r[:], AF.Exp, scale=-0.5)
                for kc in range(KC):
                    xn = self.xn[kc % 2]
                    S.tt(xn[:], y[:, kc, ysl], m[:], ALU.subtract)
                    S.tt(xn[:], xn[:], r[:], ALU.mult)
                    S.act(u2[:, kc, ysl], xn[:], AF.Silu, scale=lng(kc), bias=lnb(kc))
            S.label = f'cv_pw{hf}'
            for oc in range(KC):
                wp = wpw[npw % 2]
                npw += 1
                S.dma("pool", wp, self.d["cmpw"][a, oc].rearrange("p (k n) -> p k n", k=KC))
                for t2 in range(2):
                    tt = hf * 2 + t2
                    sl = slice(tt * TT, (tt + 1) * TT)
                    po = self.ps[(oc * 2 + t2) % 2]
                    for kc in range(KC):
                        S.mm(po[:], wp[:, kc, :], u2[:, kc, t2 * TT:(t2 + 1) * TT],
                             start=(kc == 0), stop=(kc == KC - 1), lazy=True)
                    tmp = self.sg[(oc * 2 + t2) % 2]
                    S.act(tmp[:], po[:], AF.Identity, scale=gate[:, oc:oc + 1], bias=self.cbg[:, oc:oc + 1])
                    S.tt(self.xs[:, oc, sl], self.xs[:, oc, sl], tmp[:], ALU.add)

    def _dn_mixer(self, i, s):
        S = self.S
        m = i // 2
        j = 1
        S.label = 'dn_norm'
        self._norm_mod(i, s, j)
        ps = self.ps
        gate = self.msc[:, i, s, j, 2, :]
        sm = self.dnsm
        sconv = lambda c24, k: sm[:, m * 97 + c24 * 4 + k:m * 97 + c24 * 4 + k + 1]
        og = sm[:, m * 97 + 96:m * 97 + 97]
        NTL = 16
        RG, R2, RC, RV, RW, RK, RU = 0, 4096, 16384, 20544, 32832, 41024, 53312
        gz = self.scr_b16(RG, [T])
        tmpf = lambda k: self.scr_f32(R2 + k * 2048, [TT])
        cf = self.scr_f32(R2, [T])
        dgs = self.scr_b16(R2 + 4 * 2048, [4, 128])
        cin = self.scr_b16(RC, [3 + T])
        vb = self.scr_b16(RV, [NTL, 128])
        kbg = self.scr_b16(RV + 4096, [NTL, 128])
        kd = self.scr_b16(RV + 8192, [NTL, 128])
        wqk = self.scr_b16(RW, [KC, 256])
        wvz = self.scr_b16(RW + 4096, [KC, 256])
        Rb = self.scr_b16(RW, [T])
        intraT = self.scr_b16(RW + 4096, [T])
        kTb = self.scr_b16(RK, [T])
        qTb = self.scr_b16(RK + 4096, [T])
        qgT = self.scr_b16(RK + 8192, [T])
        u = self.scr_f32(RU, [NTL, 128])
        wT = self.scr_b16(RU + 8192, [T])
        tmpr = lambda k: self.slots[:, k * TT:(k + 1) * TT]
        tmpR = tmpr
        S.label = 'dn_gates'
        wab = self.scr_b16(RW, [KC, 16])
        S.dma("pool", wab, self.d["dnwab"][m].rearrange("p (k n) -> p k n", k=KC))
        a_sb = self.scr_f32(RC, [T])
        b_sb = self.scr_f32(RC + 8192, [T])
        gc = self.scr_f32(RK, [T])
        cmk = self.scr_f32(RK + 8192, [T])
        S.dma("sp", cmk[0:8, :], self.d["cmask"][:, :])
        S.act(self.nexpa[:], self.dnhp[:, 2 * m:2 * m + 1], AF.Exp)
        S.ts(self.nexpa[:], self.nexpa[:], -1.0, ALU.mult)
        for tt in range(NT):
            sl = slice(tt * TT, (tt + 1) * TT)
            pa = ps[0]
            pb = ps[1]
            for kc in range(KC):
                S.mm(pa[0:8, :], wab[:, kc, 0:8], self.hb[:, kc, sl], start=(kc == 0), stop=(kc == KC - 1), lazy=True)
            for kc in range(KC):
                S.mm(pb[0:8, :], wab[:, kc, 8:16], self.hb[:, kc, sl], start=(kc == 0), stop=(kc == KC - 1), lazy=True)
            S.act(a_sb[0:8, sl], pa[0:8, :], AF.Exp, bias=self.dnhp[:, 2 * m + 1:2 * m + 2])
            S.act(a_sb[0:8, sl], a_sb[0:8, sl], AF.Ln, bias=1.0)
            S.act(b_sb[0:8, sl], pb[0:8, :], AF.Sigmoid)
        S.ts(a_sb[0:8, :], a_sb[0:8, :], self.nexpa[:, 0:1], ALU.mult)
        S.op("dve", lambda e: e.tensor_tensor_scan(out=gc[0:8, :], data0=cmk[0:8, :], data1=a_sb[0:8, :],
                                                   initial=0.0, op0=ALU.mult, op1=ALU.add),
             reads=[cmk[0:8, :], a_sb[0:8, :]], writes=[gc[0:8, :]])
        for n in range(NTL):
            S.transpose(ps[2][:, n * 8:(n + 1) * 8], gc[0:8, n * 128:(n + 1) * 128], self.identf[0:8, 0:8])
            S.transpose(ps[3][:, n * 8:(n + 1) * 8], b_sb[0:8, n * 128:(n + 1) * 128], self.identf[0:8, 0:8])
        S.copy(self.gtok[:], ps[2][:, 0:128])
        S.copy(self.btok[:], ps[3][:, 0:128], eng="act")
        S.mm(ps[4][:, 0:128], self.sel127[:], self.gtok[:])
        S.tt(self.ekd[:], ps[4][:, 0:128], self.gtok[:], ALU.subtract)
        S.act(self.ekd[:], self.ekd[:], AF.Exp)
        S.act(self.bkg[:], self.gtok[:], AF.Exp)
        S.tt(self.bkg[:], self.bkg[:], self.btok[:], ALU.mult)
        mUi = self.maskc[:, 0:128].unsqueeze(1).to_broadcast([128, 4, 128])
        mLs = self.maskc[:, 256:384].unsqueeze(1).to_broadcast([128, 4, 128])
        idb4 = self.identf[:, :].unsqueeze(1).to_broadcast([128, 4, 128])
        v4 = lambda ap: ap.rearrange("p (a b) -> p a b", a=4)
        nbank = [0]

        def project(wsrc, c0, evac):
            for tt in range(NT):
                sl = slice(tt * TT, (tt + 1) * TT)
                pp = ps[nbank[0] % 4]
                nbank[0] += 1
                for kc in range(KC):
                    S.mm(pp[:], wsrc[:, kc, c0:c0 + 128], self.hb[:, kc, sl], start=(kc == 0), stop=(kc == KC - 1), lazy=True)
                evac(tt, sl, pp)

        def conv_silu(c24):
            for k in range(4):
                S.ts(dgs[:, k, :], self.identb[:], sconv(c24, k), ALU.mult)
            for tt in range(NT):
                sl = slice(tt * TT, (tt + 1) * TT)
                pc = ps[4 + tt % 2]
                for k in range(4):
                    S.mm(pc[:], dgs[:, k, :], cin[:, tt * TT + k:tt * TT + k + TT], start=(k == 0), stop=(k == 3), lazy=True)
                S.act(cf[:, sl], pc[:], AF.Silu)

        def to_cin(tt, sl, pp):
            S.copy(cin[:, 3 + tt * TT:3 + (tt + 1) * TT], pp[:], eng=("act" if tt % 2 == 0 else "dve"))

        def l2norm(post):
            for tt in range(NT):
                sl = slice(tt * TT, (tt + 1) * TT)
                sq = self.sq[tt % 2]
                pst = ps[6 + tt % 2]
                S.act(sq[:], cf[:, sl], AF.Square)
                S.mm(pst[:], self.onesb[:], sq[:])
                r = self.rstd[tt % 2]
                S.act(r[:], pst[:], AF.Ln, bias=EPS)
                S.act(r[:], r[:], AF.Exp, scale=-0.5)
                post(sl, r)

        for hd in range(8):
            col = lambda n: n * 8 + hd
            S.dma("pool", wqk, self.d["dnwin"][m, hd, 0].rearrange("p (k n) -> p k n", k=KC))
            S.dma("pool", wvz, self.d["dnwin"][m, hd, 1].rearrange("p (k n) -> p k n", k=KC))
            S.memset(cin[:, 0:3], 0.0)
            S.label = f'h{hd}_z'
            project(wvz, 128, lambda tt, sl, pp: S.act(gz[:, sl], pp[:], AF.Silu))
            S.label = f'h{hd}_v'
            project(wvz, 0, to_cin)
            conv_silu(16 + hd)
            for bt in range(4):
                pv_ = ps[2 + bt % 2]
                for t in range(4):
                    n = bt * 4 + t
                    S.transpose(pv_[:, t * 128:(t + 1) * 128], cf[:, n * 128:(n + 1) * 128], self.identf[:])
                for t in range(4):
                    n = bt * 4 + t
                    if t % 2 == 0:
                        S.ts(vb[:, n, :], pv_[:, t * 128:(t + 1) * 128], self.btok[:, col(n):col(n) + 1], ALU.mult)
                    else:
                        S.act(vb[:, n, :], pv_[:, t * 128:(t + 1) * 128], AF.Identity, scale=self.btok[:, col(n):col(n) + 1])
            S.label = f'h{hd}_k'
            project(wqk, 128, to_cin)
            conv_silu(8 + hd)

            def post_k(sl, r):
                S.tt(cf[:, sl], cf[:, sl], r[:], ALU.mult)
                S.copy(kTb[:, sl], cf[:, sl], eng="act")
            l2norm(post_k)
            for bt in range(4):
                pk = ps[bt % 2]
                for t in range(4):
                    n = bt * 4 + t
                    S.transpose(pk[:, t * 128:(t + 1) * 128], cf[:, n * 128:(n + 1) * 128], self.identf[:])
                for t in range(4):
                    n = bt * 4 + t
                    S.ts(kbg[:, n, :], pk[:, t * 128:(t + 1) * 128], self.bkg[:, col(n):col(n) + 1], ALU.mult)
                    S.act(kd[:, n, :], pk[:, t * 128:(t + 1) * 128], AF.Identity, scale=self.ekd[:, col(n):col(n) + 1])
            S.label = f'h{hd}_q'
            project(wqk, 0, to_cin)
            conv_silu(hd)

            def post_q(sl, r):
                S.stt(cf[:, sl], cf[:, sl], float(128 ** -0.5), r[:], ALU.mult, ALU.mult)
                S.copy(qTb[:, sl], cf[:, sl], eng="act")
            l2norm(post_q)
            for bt in range(4):
                bsl = slice(bt * TT, (bt + 1) * TT)
                dG = tmpf(4)
                EG = tmpf(5)
                for t in range(4):
                    n = bt * 4 + t
                    S.ts(dG[:, t * 128:(t + 1) * 128], self.onesf[:], self.gtok[:, col(n):col(n) + 1], ALU.mult)
                for t in range(4):
                    tsl = slice(t * 128, (t + 1) * 128)
                    S.transpose(ps[0][:, tsl], dG[:, tsl], self.identf[:])
                S.act(EG, ps[0][:], AF.Exp)
                S.copy(self.egl[:, bt * 4:(bt + 1) * 4], EG[:, 127::128])
                S.tt(qgT[:, bsl], cf[:, bsl], EG, ALU.mult)
            wo_h = self.scr_b16(RC, [D])
            S.dma("pool", wo_h, self.d["dnwout"][m, hd])
            oT = [self.ot0[:], self.scr_f32(RC + 2048, [TT])]
            on = self.scr_b16(R2 + 4 * 2048, [T])
            S.memset(self.Sst[:], 0.0)
            S.memset(self.Sb[0][:], 0.0)

            def s4_gen(bt):
                bsl = slice(bt * TT, (bt + 1) * TT)
                dG = tmpf(0)
                for t in range(4):
                    n = bt * 4 + t
                    S.ts(dG[:, t * 128:(t + 1) * 128], self.onesf[:], self.gtok[:, col(n):col(n) + 1], ALU.mult)
                for t in range(4):
                    tsl = slice(t * 128, (t + 1) * 128)
                    S.transpose(ps[0][:, tsl], dG[:, tsl], self.identf[:])
                for t in range(4):
                    n = bt * 4 + t
                    tsl = slice(t * 128, (t + 1) * 128)
                    nsl = slice(n * 128, (n + 1) * 128)
                    S.mm(ps[1][:, tsl], kTb[:, nsl], kTb[:, nsl])
                yield
                EU = tmpf(1)
                EL = tmpf(2)
                for t in range(4):
                    n = bt * 4 + t
                    tsl = slice(t * 128, (t + 1) * 128)
                    gcol = self.gtok[:, col(n):col(n) + 1]
                    S.ts(EU[:, tsl], ps[0][:, tsl], gcol, ALU.subtract, 0.0, ALU.min)
                    S.ts(EL[:, tsl], ps[0][:, tsl], gcol, ALU.subtract, 0.0, ALU.max)
                S.act(EU, EU, AF.Exp)
                S.act(EL, EL, AF.Exp, scale=-1.0)
                for t in range(4):
                    n = bt * 4 + t
                    tsl = slice(t * 128, (t + 1) * 128)
                    nsl = slice(n * 128, (n + 1) * 128)
                    S.mm(ps[0][:, tsl], kTb[:, nsl], qTb[:, nsl])
                yield
                S.tt(v4(EL), v4(EL), mLs, ALU.mult)
                S.tt(v4(EU), v4(EU), mUi, ALU.mult)
                Af = tmpf(3)
                for t in range(4):
                    n = bt * 4 + t
                    tsl = slice(t * 128, (t + 1) * 128)
                    S.stt(Af[:, tsl], ps[1][:, tsl], self.btok[:, col(n):col(n) + 1], EL[:, tsl], ALU.mult, ALU.mult)
                for t in range(4):
                    tsl = slice(t * 128, (t + 1) * 128)
                    S.transpose(ps[1][:, tsl], Af[:, tsl], self.identf[:])
                S.tt(intraT[:, bsl], ps[0][:], EU, ALU.mult)
                yield
                slotA = [tmpR(0), tmpR(1)]
                slotU = [tmpR(2), tmpR(3)]
                P = tmpR(4)
                Pf = tmpr(4)
                S.copy(slotA[0], Af, eng="dve")
                S.copy(slotU[0], ps[1][:], eng="dve")
                S.tt(v4(P), idb4, v4(ps[1][:]), ALU.subtract)
                pA, pU, pP = ps[2], ps[3], ps[4]

                def square(l):
                    A_c, U_c = slotA[l % 2], slotU[l % 2]
                    for t in range(4):
                        tsl = slice(t * 128, (t + 1) * 128)
                        S.mm(pA[:, tsl], U_c[:, tsl], A_c[:, tsl])
                    if l + 1 < 6:
                        for t in range(4):
                            tsl = slice(t * 128, (t + 1) * 128)
                            S.mm(pU[:, tsl], A_c[:, tsl], U_c[:, tsl])

                def evac(l):
                    S.copy(slotA[(l + 1) % 2], pA[:], eng="act")
                    if l + 1 < 6:
                        S.copy(slotU[(l + 1) % 2], pU[:], eng="dve")
                square(0)
                evac(0)
                yield
                for l in range(1, 7):
                    A_l = slotA[l % 2]
                    if l < 6:
                        square(l)
                    for t in range(4):
                        tsl = slice(t * 128, (t + 1) * 128)
                        S.mm(pP[:, tsl], A_l[:, tsl], P[:, tsl])
                    if l < 6:
                        evac(l)
                        S.tt(P, pP[:], Pf, ALU.add)
                    else:
                        S.tt(Rb[:, bsl], pP[:], Pf, ALU.add)
                    yield
                for t in range(4):
                    n = bt * 4 + t
                    tsl = slice(t * 128, (t + 1) * 128)
                    nsl = slice(n * 128, (n + 1) * 128)
                    S.mm(ps[0][:, tsl], Rb[:, nsl], vb[:, n, :])
                    S.mm(ps[1][:, tsl], kbg[:, n, :], Rb[:, nsl])
                S.copy(u[:, bt * 4:(bt + 1) * 4, :].rearrange("p a b -> p (a b)"), ps[0][:], eng="act")
                S.copy(wT[:, bsl], ps[1][:])
                yield

            def scan_gen(bt):
                bsl = slice(bt * TT, (bt + 1) * TT)
                po = ps[6]
                for t in range(4):
                    n = bt * 4 + t
                    nsl = slice(n * 128, (n + 1) * 128)
                    tsl = slice(t * 128, (t + 1) * 128)
                    Sb = self.Sb[n % 2]
                    Sbn = self.Sb[(n + 1) % 2]
                    vn = self.vnew[n % 2]
                    S.mm(ps[5][:, 0:128], wT[:, nsl], Sb[:])
                    S.mm(po[:, tsl], Sb[:], qgT[:, nsl], start=True, stop=False)
                    yield
                    S.tt(vn[:], u[:, n, :], ps[5][:, 0:128], ALU.subtract)
                    S.mm(po[:, tsl], vn[:], intraT[:, nsl], start=False, stop=True)
                    S.mm(ps[5][:, 128:256], kd[:, n, :], vn[:])
                    S.stt(Sbn[:], self.Sst[:], self.egl[:, n:n + 1], ps[5][:, 128:256], ALU.mult, ALU.add)
                    S.stt(self.Sst[:], self.Sst[:], self.egl[:, n:n + 1], ps[5][:, 128:256], ALU.mult, ALU.add)
                    yield
                sq = self.sq[bt % 2]
                pst = ps[7]
                S.act(sq[:], po[:], AF.Square)
                S.mm(pst[:], self.onesb[:], sq[:])
                r = self.rstd[bt % 2]
                S.act(r[:], pst[:], AF.Ln, scale=1.0 / 128, bias=EPS)
                S.act(r[:], r[:], AF.Exp, scale=-0.5)
                ot = oT[bt % 2]
                S.tt(ot, po[:], r[:], ALU.mult)
                S.stt(on[:, bsl], ot, og, gz[:, bsl], ALU.mult, ALU.mult)
                yield

            def s6_gen(tt):
                sl = slice(tt * TT, (tt + 1) * TT)
                for oc in range(KC):
                    pp = ps[7]
                    S.mm(pp[:], wo_h[:, oc * 128:(oc + 1) * 128], on[:, sl])
                    S.stt(self.xs[:, oc, sl], pp[:], gate[:, oc:oc + 1], self.xs[:, oc, sl], ALU.mult, ALU.add)
                    yield

            for k in range(6):
                S.label = f'h{hd}_pipe{k}'
                gens = []
                if k < 4:
                    gens.append(s4_gen(k))
                if 1 <= k <= 4:
                    gens.append(scan_gen(k - 1))
                if 2 <= k <= 5:
                    gens.append(s6_gen(k - 2))
                while gens:
                    for g in list(gens):
                        try:
                            next(g)
                        except StopIteration:
                            gens.remove(g)

    def _finish(self, s):
        S = self.S
        toks = []
        if self.do_final:
            self._norm(lambda kc: self.fg[:, kc:kc + 1], lambda kc: 0.0,
                       lambda kc, tt: self.xs[:, kc, tt * TT:(tt + 1) * TT])
        for kc in range(KC):
            toks.append(S.dma("sp", self.out[s, kc * 128:(kc + 1) * 128, :], self.xs[:, kc, :]))
        return toks


def _fm(v):
    v = np.asarray(v, np.float32)
    lead = v.shape[:-1]
    return np.ascontiguousarray(np.moveaxis(v.reshape(lead + (KC, 128)), -1, 0))


def prep_shared(inp, layers):
    sh = {}
    L = list(layers)
    sh["ng"] = _fm(inp["norm_g"]).reshape(128, DEPTH * 3 * KC)
    wa = np.asarray(inp["w_ada"], np.float32)
    wa = wa.reshape(DEPTH, KC, 128, 72, 128).transpose(0, 3, 2, 1, 4)
    sh["wada"] = np.ascontiguousarray(wa[L]).reshape(len(L), 72, 128, KC * 128)
    ba = np.asarray(inp["b_ada"], np.float32).reshape(DEPTH, 72, 128)
    sh["bada"] = np.ascontiguousarray(ba.transpose(2, 0, 1)).reshape(128, DEPTH * 72)
    wi = np.asarray(inp["w_ffn_in"], np.float32)
    wi = wi.reshape(DEPTH, 2, KC, 128, 2, FC, 128)
    wi = wi.transpose(0, 1, 5, 3, 2, 4, 6)
    sh["win"] = np.ascontiguousarray(wi[L]).reshape(len(L), 2, FC, 128, KC * 256)
    wo = np.asarray(inp["w_ffn_out"], np.float32)
    wo = wo.reshape(DEPTH, 2, 2, 11, 128, KC, 128)
    wo = wo.transpose(0, 1, 2, 5, 4, 3, 6)
    sh["wout"] = np.ascontiguousarray(wo[L]).reshape(len(L), 2, 2, KC, 128, 11 * 128)
    sh["fg"] = _fm(inp["final_g"]).reshape(128, KC)
    wg = np.asarray(inp["cm_w_glu"], np.float32).reshape(2, KC, 128, 2, KC, 128)
    sh["cmglu"] = np.ascontiguousarray(wg.transpose(0, 4, 2, 1, 3, 5)).reshape(2, KC, 128, KC * 256)
    wp = np.asarray(inp["cm_w_pw"], np.float32).reshape(2, KC, 128, KC, 128)
    sh["cmpw"] = np.ascontiguousarray(wp.transpose(0, 3, 2, 1, 4)).reshape(2, KC, 128, KC * 128)
    sm = np.zeros((128, 2, 296), np.float32)
    bg = _fm(np.asarray(inp["cm_b_glu"], np.float32).reshape(2, 2, D))
    sm[:, :, 0:8] = bg[:, :, 0, :]
    sm[:, :, 8:16] = bg[:, :, 1, :]
    wd = _fm(inp["cm_w_dw"])
    sm[:, :, 16:264] = wd.transpose(0, 1, 3, 2).reshape(128, 2, 248)
    sm[:, :, 264:272] = _fm(inp["cm_b_dw"])
    sm[:, :, 272:280] = _fm(inp["cm_ln_g"])
    sm[:, :, 280:288] = _fm(inp["cm_ln_b"])
    sm[:, :, 288:296] = _fm(inp["cm_b_pw"])
    sh["cmsm"] = sm.reshape(128, 2 * 296)
    W = 1024
    dw = np.asarray(inp["dn_w_in"], np.float32)
    parts = [dw[:, :, c * W:(c + 1) * W].reshape(2, KC, 128, 8, 128) for c in range(4)]
    qk = np.stack([parts[0], parts[1]], axis=4)
    vz = np.stack([parts[2], parts[3]], axis=4)
    both = np.stack([qk, vz], axis=4)
    sh["dnwin"] = np.ascontiguousarray(both.transpose(0, 3, 4, 2, 1, 5, 6)).reshape(2, 8, 2, 128, KC * 256)
    ab = dw[:, :, 4 * W:4 * W + 16].reshape(2, KC, 128, 16)
    sh["dnwab"] = np.ascontiguousarray(ab.transpose(0, 2, 1, 3)).reshape(2, 128, KC * 16)
    sh["dnwout"] = np.ascontiguousarray(np.asarray(inp["dn_w_out"], np.float32).reshape(2, 8, 128, D))
    dsm = np.zeros((128, 2, 97), np.float32)
    sc = np.asarray(inp["dn_w_sconv"], np.float32).reshape(2, 4, 24, 128)
    dsm[:, :, 0:96] = sc.transpose(3, 0, 2, 1).reshape(128, 2, 96)
    dsm[:, :, 96] = np.asarray(inp["dn_o_g"], np.float32).T
    sh["dnsm"] = dsm.reshape(128, 2 * 97)
    hp = np.zeros((8, 4), np.float32)
    for mm_ in range(2):
        hp[:, 2 * mm_] = np.asarray(inp["dn_a_log"], np.float32)[mm_]
        hp[:, 2 * mm_ + 1] = np.asarray(inp["dn_dt_bias"], np.float32)[mm_]
    sh["dnhp"] = hp
    pi = np.arange(128)[:, None]
    fi = np.arange(128)[None, :]
    sh["maskc"] = np.concatenate([(fi >= pi), (fi > pi), (pi > fi)], axis=1).astype(np.float32)
    s127 = np.zeros((128, 128), np.float32)
    s127[127, :] = 1.0
    sh["sel127"] = s127
    cm = np.ones((8, T), np.float32)
    cm[:, ::128] = 0.0
    sh["cmask"] = cm
    sh["identf"] = np.eye(128, dtype=np.float32)
    return sh


def prep_core(inp, c):
    m = {}
    x = np.asarray(inp["x"], np.float32)
    m["xT"] = np.ascontiguousarray(x[2 * c:2 * c + 2].transpose(0, 2, 1))
    cc = np.asarray(inp["c"], np.float32)[2 * c:2 * c + 2]
    m["cT"] = np.ascontiguousarray(cc.reshape(2, KC, 128).transpose(2, 1, 0)).reshape(128, KC * 2)
    return m


_PROG_CACHE = {}


def run(inputs, stages=None, nseq=2, do_final=True, ncores=NCORES, debug=False):
    key = (None if stages is None else tuple(stages), nseq, do_final, debug)
    if key not in _PROG_CACHE:
        _PROG_CACHE[key] = Prog(stages, nseq, do_final, debug)
    prog = _PROG_CACHE[key]
    sh = prep_shared(inputs, prog.layers)
    in_maps = []
    for c in range(ncores):
        m = dict(sh)
        m.update(prep_core(inputs, c))
        in_maps.append(m)
    res = run_bass_kernel_spmd(prog.nc, in_maps, core_ids=list(range(ncores)))
    if debug:
        global LAST_DBG
        LAST_DBG = np.asarray(res.results[0]["dbg"])
    B = 2 * ncores
    out = np.empty((B, T, D), np.float32)
    for c in range(ncores):
        o = np.asarray(res.results[c]["outT"])
        out[2 * c:2 * c + 2] = o.transpose(0, 2, 1)
    return out


def kernel(**inputs):
    return run(inputs)
```

```python
import contextlib
import numpy as np
import concourse.bass as bass
import concourse.mybir as mybir
from concourse.bass_utils import run_bass_kernel_spmd

F32 = mybir.dt.float32
BF16 = mybir.dt.bfloat16
AF = mybir.ActivationFunctionType
ALU = mybir.AluOpType

D = 1024
T = 2048
KC = 8
TT = 512
NT = T // TT
FF = 2816
FC = 22
DEPTH = 4
EPS = 1e-6
NCORES = 8
ENGS = ("pe", "act", "dve", "pool", "sp")
F32R = mybir.dt.float32r
_ESZ = {F32: 4, BF16: 2, F32R: 4}


def _res(ap):
    a = ap.ap
    ps = a[0][0]
    off = ap.offset % ps if ps else ap.offset
    span = 1
    for s, n in a[1:]:
        span += (n - 1) * abs(s)
    e = _ESZ[ap.dtype]
    if "PSUM" in str(ap.space).upper():
        return ("@" + ap.tensor.name, 0, 1 << 20)
    return (ap.tensor.name, off * e, (off + span) * e)


class Sched:
    NDMA = 6

    def __init__(self, nc, stack):
        self.nc = nc
        self.sems = {}
        for e in ENGS:
            self.sems[("p", e)] = stack.enter_context(nc.semaphore(f"prog_{e}"))
        for q in ("sp", "pool"):
            for i in range(self.NDMA):
                self.sems[("d", q, i)] = stack.enter_context(nc.semaphore(f"dma_{q}_{i}"))
        self.cnt = {k: 0 for k in self.sems}
        self.seen = {e: {} for e in ENGS}
        self.prog = {e: [] for e in ENGS}
        self.ents = {}
        self.dma_i = {"sp": 0, "pool": 0}
        self.dma_last = {}
        self.nops = {e: 0 for e in ENGS}
        self.label = ''
        self.labels = {e: [] for e in ENGS}

    def _collect(self, e, reads, writes, extra=()):
        need = {}

        def add(tok):
            if tok is not None and need.get(tok[0], 0) < tok[1]:
                need[tok[0]] = tok[1]
        own = ("p", e)
        for (name, lo, hi) in reads:
            for ent in self.ents.get(name, ()):
                if ent[0] < hi and lo < ent[1]:
                    add(ent[2])
                    if name[0] == "@":
                        for k, v in ent[3].items():
                            if k != own:
                                add((k, v))
        for (name, lo, hi) in writes:
            for ent in self.ents.get(name, ()):
                if ent[0] < hi and lo < ent[1]:
                    add(ent[2])
                    for k, v in ent[3].items():
                        add((k, v))
        for t in extra:
            add(t)
        waits = []
        seen = self.seen[e]
        for k, v in need.items():
            if e == "pe" and k == ("p", "pe"):
                continue
            if seen.get(k, 0) >= v:
                continue
            seen[k] = v
            waits.append((k, v))
        return waits

    def _commit(self, reads, writes, tok):
        k, v = tok
        for (name, lo, hi) in reads:
            ents = self.ents.setdefault(name, [])
            covered = False
            for ent in ents:
                if ent[0] < hi and lo < ent[1]:
                    if ent[3].get(k, 0) < v:
                        ent[3][k] = v
                    if ent[0] <= lo and hi <= ent[1]:
                        covered = True
            if not covered:
                ents.append([lo, hi, None, {k: v}])
        for (name, lo, hi) in writes:
            ents = self.ents.setdefault(name, [])
            new = [ent for ent in ents
                   if not (ent[0] < hi and lo < ent[1] and lo <= ent[0] and ent[1] <= hi)]
            new.append([lo, hi, tok, {}])
            self.ents[name] = new

    def op(self, e, fn, reads=(), writes=(), signal=True):
        reads = [_res(a) for a in reads]
        writes = [_res(a) for a in writes]
        waits = self._collect(e, reads, writes)
        pk = ("p", e)
        if signal:
            self.cnt[pk] += 1
            tok = (pk, self.cnt[pk])
        else:
            tok = (pk, self.cnt[pk] + 1)
        self._commit(reads, writes, tok)
        sems = self.sems
        self.nops[e] += 1
        self.labels[e].append(self.label)

        def emit(engine, waits=waits, fn=fn, signal=signal, pk=pk):
            for k, v in waits:
                engine.wait_ge(sems[k], v)
            ins = fn(engine)
            if signal:
                ins.then_inc(sems[pk], 1)
        self.prog[e].append(emit)
        return tok

    def dma(self, q, out, in_, **kw):
        reads = [_res(in_)] if _is_onchip(in_) else []
        writes = [_res(out)] if _is_onchip(out) else []
        i = self.dma_i[q] % self.NDMA
        self.dma_i[q] += 1
        dk = ("d", q, i)
        extra = []
        if dk in self.dma_last:
            extra.append(self.dma_last[dk])
        waits = self._collect(q, reads, writes, extra)
        self.cnt[dk] += 16
        tok = (dk, self.cnt[dk])
        self.dma_last[dk] = tok
        self._commit(reads, writes, tok)
        sems = self.sems
        self.nops[q] += 1

        def emit(engine, waits=waits, out=out, in_=in_, dk=dk, kw=kw):
            for k, v in waits:
                engine.wait_ge(sems[k], v)
            engine.dma_start(out=out, in_=in_, **kw).then_inc(sems[dk], 16)
        self.prog[q].append(emit)
        return tok

    def wait_tokens(self, e, toks):
        sems = self.sems
        ws = []
        for k, v in toks:
            if self.seen[e].get(k, 0) >= v:
                continue
            self.seen[e][k] = v
            ws.append((k, v))

        def emit(engine, ws=ws):
            for k, v in ws:
                engine.wait_ge(sems[k], v)
        self.prog[e].append(emit)

    def replay(self, block):
        prog = self.prog

        @block.sync
        def _(eng):
            for f in prog["sp"]:
                f(eng)

        @block.scalar
        def _(eng):
            for f in prog["act"]:
                f(eng)

        @block.vector
        def _(eng):
            for f in prog["dve"]:
                f(eng)

        @block.gpsimd
        def _(eng):
            for f in prog["pool"]:
                f(eng)

        @block.tensor
        def _(eng):
            for f in prog["pe"]:
                f(eng)

    def mm(self, out, lhsT, rhs, start=True, stop=True, lazy=False):
        self.op("pe", lambda e: e.matmul(out, lhsT=lhsT, rhs=rhs, start=start, stop=stop),
                reads=[lhsT, rhs], writes=[out], signal=(stop or not lazy))

    def transpose(self, out, in_, ident):
        self.op("pe", lambda e: e.transpose(out, in_, ident), reads=[in_, ident], writes=[out])

    def act(self, out, in_, func, scale=1.0, bias=0.0):
        rd = [in_]
        if not isinstance(scale, (int, float)):
            rd.append(scale)
        if not isinstance(bias, (int, float)):
            rd.append(bias)
        self.op("act", lambda e: e.activation(out=out, in_=in_, func=func, scale=scale, bias=bias),
                reads=rd, writes=[out])

    def tt(self, out, in0, in1, op, eng="dve"):
        self.op(eng, lambda e: e.tensor_tensor(out=out, in0=in0, in1=in1, op=op),
                reads=[in0, in1], writes=[out])

    def ts(self, out, in0, s1, op0, s2=None, op1=None, eng="dve"):
        rd = [in0]
        if not isinstance(s1, (int, float)):
            rd.append(s1)
        if s2 is not None and not isinstance(s2, (int, float)):
            rd.append(s2)
        if op1 is None:
            self.op(eng, lambda e: e.tensor_scalar(out=out, in0=in0, scalar1=s1, scalar2=None, op0=op0),
                    reads=rd, writes=[out])
        else:
            self.op(eng, lambda e: e.tensor_scalar(out=out, in0=in0, scalar1=s1, scalar2=s2, op0=op0, op1=op1),
                    reads=rd, writes=[out])

    def stt(self, out, in0, scalar, in1, op0, op1):
        rd = [in0, in1]
        if not isinstance(scalar, (int, float)):
            rd.append(scalar)
        self.op("dve", lambda e: e.scalar_tensor_tensor(out=out, in0=in0, scalar=scalar, in1=in1, op0=op0, op1=op1),
                reads=rd, writes=[out])

    def copy(self, out, in_, eng="dve"):
        if eng == "act":
            self.op("act", lambda e: e.copy(out=out, in_=in_), reads=[in_], writes=[out])
        else:
            self.op(eng, lambda e: e.tensor_copy(out=out, in_=in_), reads=[in_], writes=[out])

    def memset(self, ap, val, eng="dve"):
        self.op(eng, lambda e: e.memset(ap, val), reads=[], writes=[ap])


def _is_onchip(ap):
    return "DRAM" not in str(ap.space).upper()


class Prog:
    def __init__(self, stages=None, nseq=2, do_final=True, debug=False):
        self.debug = debug
        if stages is None:
            stages = [(i, j) for i in range(DEPTH) for j in range(3)]
        self.stages = list(stages)
        self.layers = sorted(set(i for i, _ in self.stages))
        self.nseq = nseq
        self.do_final = do_final
        self.nc = bass.Bass("TRN2", target_bir_lowering=False)
        self._build()

    def _dram(self):
        nc = self.nc

        def din(name, shape):
            return nc.dram_tensor(name, list(shape), F32, kind="ExternalInput").ap()
        d = {}
        d["xT"] = din("xT", [2, D, T])
        d["cT"] = din("cT", [128, KC * 2])
        NL = len(self.layers)
        d["ng"] = din("ng", [128, DEPTH * 3 * KC])
        d["wada"] = din("wada", [NL, 72, 128, KC * 128])
        d["bada"] = din("bada", [128, DEPTH * 72])
        d["win"] = din("win", [NL, 2, FC, 128, KC * 256])
        d["wout"] = din("wout", [NL, 2, 2, KC, 128, 11 * 128])
        d["fg"] = din("fg", [128, KC])
        d["cmglu"] = din("cmglu", [2, KC, 128, KC * 256])
        d["cmpw"] = din("cmpw", [2, KC, 128, KC * 128])
        d["cmsm"] = din("cmsm", [128, 2 * 296])
        d["dnwin"] = din("dnwin", [2, 8, 2, 128, KC * 256])
        d["dnwab"] = din("dnwab", [2, 128, KC * 16])
        d["dnwout"] = din("dnwout", [2, 8, 128, D])
        d["dnsm"] = din("dnsm", [128, 2 * 97])
        d["dnhp"] = din("dnhp", [8, 4])
        d["maskc"] = din("maskc", [128, 3 * 128])
        d["sel127"] = din("sel127", [128, 128])
        d["cmask"] = din("cmask", [8, T])
        d["identf"] = din("identf", [128, 128])
        self.out = nc.dram_tensor("outT", [2, D, T], F32, kind="ExternalOutput").ap()
        self.dbg = nc.dram_tensor("dbg", [128, 8192], F32, kind="ExternalOutput").ap() if self.debug else None
        self.d = d

    def _build(self):
        nc = self.nc
        self._dram()
        with contextlib.ExitStack() as st:
            def sb(name, shape, dt):
                return st.enter_context(nc.sbuf_tensor(name, list(shape), dt))
            self.xs = sb("xs", [128, KC, T], F32)
            self.hb = sb("hb", [128, KC, T], BF16)
            self.SCR_BYTES = 65 * 1024
            self.scr = sb("scr", [128, self.SCR_BYTES // 4], F32)
            self.scr_bf = self.scr[:, :].bitcast(BF16)
            self.slots = sb("slots", [128, 5 * TT], F32)
            self.ot0 = sb("ot0", [128, TT], F32)
            self.mod = sb("mod", [128, DEPTH, 2, 72], F32)
            self.msc = sb("msc", [128, DEPTH, 2, 3, 3, KC], F32)
            self.ct = sb("ct", [128, KC * 2], F32)
            self.cs = sb("cs", [128, KC * 2], F32)
            self.ng = sb("ngs", [128, DEPTH * 3 * KC], F32)
            self.bada = sb("badas", [128, DEPTH * 72], F32)
            self.fg = sb("fgs", [128, KC], F32)
            self.identf = sb("identfs", [128, 128], F32)
            self.onesb = sb("onesb", [128, 128], BF16)
            self.identb = sb("identb", [128, 128], BF16)
            self.cmsm = sb("cmsms", [128, 2 * 296], F32)
            self.cbg = sb("cbg", [128, KC], F32)
            self.dnsm = sb("dnsms", [128, 2 * 97], F32)
            self.dnhp = sb("dnhps", [8, 4], F32)
            self.maskc = sb("maskcs", [128, 3 * 128], F32)
            self.sel127 = sb("sel127s", [128, 128], F32)
            self.onesf = sb("onesf", [128, 128], F32)
            self.nonesf = sb("nonesf", [128, 128], F32)
            self.nexpa = sb("nexpa", [8, 1], F32)
            self.gtok = sb("gtok", [128, 128], F32)
            self.btok = sb("btok", [128, 128], F32)
            self.ekd = sb("ekd", [128, 128], F32)
            self.bkg = sb("bkg", [128, 128], F32)
            self.egl = sb("egl", [128, 16], F32)
            self.Sst = sb("Sst", [128, 128], F32)
            self.Sb = [sb(f"Sb{i}", [128, 128], BF16) for i in range(2)]
            self.vnew = [sb(f"vnew{i}", [128, 128], BF16) for i in range(2)]
            self.sqb = [sb(f"sqb{i}", [128, TT], BF16) for i in range(2)]
            self.sq = [sb(f"sq{i}", [128, TT], BF16) for i in range(2)]
            self.rstd = [sb(f"rstd{i}", [128, TT], F32) for i in range(2)]
            self.xn = [sb(f"xn{i}", [128, TT], F32) for i in range(2)]
            self.sg = [sb(f"sg{i}", [128, TT], F32) for i in range(2)]
            self.ps = [st.enter_context(nc.psum_tensor(f"ps{i}", [128, TT], F32)) for i in range(8)]
            self.S = Sched(nc, st)
            block = st.enter_context(nc.Block())
            self._emit_all()
            self.S.replay(block)

    def scr_f32(self, byte_off, shape):
        n = int(np.prod(shape))
        v = self.scr[:, byte_off // 4: byte_off // 4 + n]
        return self._shape(v, shape)

    def scr_b16(self, byte_off, shape):
        n = int(np.prod(shape))
        v = self.scr_bf[:, byte_off // 2: byte_off // 2 + n]
        return self._shape(v, shape)

    @staticmethod
    def _shape(v, shape):
        if len(shape) == 1:
            return v
        if len(shape) == 2:
            return v.rearrange("p (a b) -> p a b", a=shape[0])
        if len(shape) == 3:
            return v.rearrange("p (a b c) -> p a b c", a=shape[0], b=shape[1])
        raise ValueError

    def _emit_all(self):
        S = self.S
        d = self.d
        S.dma("sp", self.ct[:], d["cT"][:, :])
        S.dma("sp", self.ng[:], d["ng"][:, :])
        S.dma("sp", self.bada[:], d["bada"][:, :])
        S.dma("sp", self.fg[:], d["fg"][:, :])
        S.dma("sp", self.identf[:], d["identf"][:, :])
        S.memset(self.onesb[:], 1.0)
        S.dma("sp", self.cmsm[:], d["cmsm"][:, :])
        S.copy(self.identb[:], self.identf[:])
        S.dma("sp", self.dnsm[:], d["dnsm"][:, :])
        S.dma("sp", self.dnhp[:], d["dnhp"][:, :])
        S.dma("sp", self.maskc[:], d["maskc"][:, :])
        S.dma("sp", self.sel127[:], d["sel127"][:, :])
        S.memset(self.onesf[:], 1.0)
        S.memset(self.nonesf[:], -1.0)
        S.act(self.cs[:], self.ct[:], AF.Silu)
        self._adaln()
        toks = []
        for s in range(self.nseq):
            self._load_x(s)
            for (i, j) in self.stages:
                if j == 0:
                    self._ffn(i, s, 0)
                elif j == 1:
                    self._mixer(i, s)
                elif j == 2:
                    self._ffn(i, s, 1)
                elif j == 8:
                    self._norm_mod(i, s, 0)
            if self.debug and s == 0:
                toks += self._dump()
            toks += self._finish(s)
        S.wait_tokens("sp", toks)

    def _load_x(self, s):
        for kc in range(KC):
            self.S.dma("sp", self.xs[:, kc, :], self.d["xT"][s, kc * 128:(kc + 1) * 128, :])

    def _adaln(self):
        S = self.S
        bufs = [self.scr_f32(0, [4, KC * 128]), self.scr_f32(16384, [4, KC * 128])]
        for i in self.layers:
            ps = self.ps[7]
            psv = ps[:, 0:144].rearrange("p (s c) -> p c s", s=2)
            for g in range(18):
                buf = bufs[g % 2]
                S.dma("sp", buf, self.d["wada"][self.layers.index(i), g * 4:(g + 1) * 4].rearrange("c p n -> p c n"))
                for j in range(4):
                    ch = g * 4 + j
                    for kc in range(KC):
                        S.mm(psv[:, ch, :], buf[:, j, kc * 128:(kc + 1) * 128],
                             self.cs[:, kc * 2:(kc + 1) * 2], start=(kc == 0), stop=(kc == KC - 1))
            for s in range(2):
                S.tt(self.mod[:, i, s, :], ps[:, s * 72:(s + 1) * 72], self.bada[:, i * 72:(i + 1) * 72], ALU.add)
            for s in range(2):
                for j in range(3):
                    m = self.mod[:, i, s, j * 24:(j + 1) * 24]
                    ngv = self.ng[:, (i * 3 + j) * KC:(i * 3 + j + 1) * KC]
                    S.stt(self.msc[:, i, s, j, 0, :], m[:, 8:16], 1.0, ngv, ALU.add, ALU.mult)
                    S.copy(self.msc[:, i, s, j, 1, :], m[:, 0:8])
                    rw = 1.0 if j == 1 else 0.5
                    S.ts(self.msc[:, i, s, j, 2, :], m[:, 16:24], 1.0, ALU.add, rw, ALU.mult)

    def _norm(self, gs_fn, shift_fn, out_fn, tts=range(NT)):
        S = self.S
        for tt in tts:
            sl = slice(tt * TT, (tt + 1) * TT)
            ps = self.ps[6 + (tt % 2)]
            for kc in range(KC):
                if kc % 2 == 0:
                    sq = self.sq[(kc // 2) % 2]
                    S.act(sq[:], self.xs[:, kc, sl], AF.Square)
                else:
                    sq = self.sqb[(kc // 2) % 2]
                    S.tt(sq[:], self.xs[:, kc, sl], self.xs[:, kc, sl], ALU.mult, eng="pool")
                S.mm(ps[:], self.onesb[:], sq[:], start=(kc == 0), stop=(kc == KC - 1))
            r = self.rstd[tt % 2]
            S.act(r[:], ps[:], AF.Ln, scale=1.0 / D, bias=EPS)
            S.act(ps[:], r[:], AF.Exp, scale=-0.5)
            for kc in range(KC):
                xn = self.xn[kc % 2]
                S.tt(xn[:], self.xs[:, kc, sl], ps[:], ALU.mult)
                S.act(out_fn(kc, tt), xn[:], AF.Identity, scale=gs_fn(kc), bias=shift_fn(kc))

    def _norm_mod(self, i, s, j):
        self._norm(lambda kc: self.msc[:, i, s, j, 0, kc:kc + 1],
                   lambda kc: self.msc[:, i, s, j, 1, kc:kc + 1],
                   lambda kc, tt: self.hb[:, kc, tt * TT:(tt + 1) * TT])

    def _ffn(self, i, s, which):
        S = self.S
        j = 0 if which == 0 else 2
        self._norm_mod(i, s, j)
        hid = self.scr_b16(0, [11, T])
        winb = [self.scr_b16(45056 + b * 4096, [KC, 256]) for b in range(3)]
        woutb = [self.scr_b16(45056 + 12288 + b * 2816, [11, 128]) for b in range(3)]
        win_d = self.d["win"]
        wout_d = self.d["wout"]
        gatec = lambda kc: self.msc[:, i, s, j, 2, kc:kc + 1]
        n_in = 0
        n_out = 0
        for half in range(2):
            for f in range(11):
                fc = half * 11 + f
                wb = winb[n_in % 3]
                n_in += 1
                S.dma("pool", wb, win_d[self.layers.index(i), which, fc].rearrange("p (k n) -> p k n", k=KC))
                for tt in range(NT):
                    sl = slice(tt * TT, (tt + 1) * TT)
                    pg = self.ps[(f * NT + tt) % 2]
                    pu = self.ps[2 + (f * NT + tt) % 2]
                    for kc in range(KC):
                        S.mm(pg[:], wb[:, kc, 0:128], self.hb[:, kc, sl], start=(kc == 0), stop=(kc == KC - 1), lazy=True)
                    for kc in range(KC):
                        S.mm(pu[:], wb[:, kc, 128:256], self.hb[:, kc, sl], start=(kc == 0), stop=(kc == KC - 1), lazy=True)
                    sg = self.sg[(f * NT + tt) % 2]
                    S.act(sg[:], pg[:], AF.Silu)
                    S.tt(hid[:, f, sl], pu[:], sg[:], ALU.mult)
            for dc in range(KC):
                wo = woutb[n_out % 3]
                n_out += 1
                S.dma("pool", wo, wout_d[self.layers.index(i), which, half, dc].rearrange("p (k n) -> p k n", k=11))
                for tt in range(NT):
                    sl = slice(tt * TT, (tt + 1) * TT)
                    po = self.ps[4 + (dc * NT + tt) % 2]
                    for f in range(11):
                        S.mm(po[:], wo[:, f, :], hid[:, f, sl], start=(f == 0), stop=(f == 10), lazy=True)
                    S.stt(self.xs[:, dc, sl], po[:], gatec(dc), self.xs[:, dc, sl], ALU.mult, ALU.add)

    def _mixer(self, i, s):
        if i % 2 == 0:
            self._conv_mixer(i, s)
        else:
            self._dn_mixer(i, s)

    def _conv_mixer(self, i, s):
        S = self.S
        a = i // 2
        j = 1
        S.label = 'cv_norm'
        self._norm_mod(i, s, j)
        S.label = 'cv_glu'
        base = a * 296
        sm = self.cmsm
        bga = lambda oc: sm[:, base + oc:base + oc + 1]
        bgb = lambda oc: sm[:, base + 8 + oc:base + 8 + oc + 1]
        wdw = lambda oc, k: sm[:, base + 16 + oc * 31 + k:base + 16 + oc * 31 + k + 1]
        bdw = lambda oc: sm[:, base + 264 + oc:base + 264 + oc + 1]
        lng = lambda oc: sm[:, base + 272 + oc:base + 272 + oc + 1]
        lnb = lambda oc: sm[:, base + 280 + oc:base + 280 + oc + 1]
        gate = self.msc[:, i, s, j, 2, :]
        S.tt(self.cbg[:], sm[:, base + 288:base + 296], gate, ALU.mult)
        UW = 30 + T
        u = self.scr_b16(0, [KC, UW])
        wgl = [self.scr_b16(33280 + b * 4096, [KC, 256]) for b in range(2)]
        wpw = [self.scr_b16(33280 + b * 2048, [KC, 128]) for b in range(2)]
        diag = [self.scr_b16(33280, [31, 128]), self.scr_b16(57856, [31, 128])]
        sgt = [self.sg[0][:], self.sg[1][:]]
        u2 = self.scr_b16(41472, [KC, 1024])
        yv = self.hb[:, :, :].rearrange("p a b -> p (a b)").bitcast(F32)
        y = yv.rearrange("p (a b) -> p a b", a=KC)
        S.memset(u[:, :, 0:30], 0.0)
        for oc in range(KC):
            wb = wgl[oc % 2]
            S.dma("pool", wb, self.d["cmglu"][a, oc].rearrange("p (k n) -> p k n", k=KC))
            for tt in range(NT):
                sl = slice(tt * TT, (tt + 1) * TT)
                pa = self.ps[(oc * NT + tt) % 2]
                pb = self.ps[2 + (oc * NT + tt) % 2]
                for kc in range(KC):
                    S.mm(pa[:], wb[:, kc, 0:128], self.hb[:, kc, sl], start=(kc == 0), stop=(kc == KC - 1), lazy=True)
                for kc in range(KC):
                    S.mm(pb[:], wb[:, kc, 128:256], self.hb[:, kc, sl], start=(kc == 0), stop=(kc == KC - 1), lazy=True)
                sg = sgt[(oc * NT + tt) % 2]
                S.act(sg, pb[:], AF.Sigmoid, bias=bgb(oc))
                S.stt(u[:, oc, 30 + tt * TT:30 + (tt + 1) * TT], pa[:], bga(oc), sg, ALU.add, ALU.mult)
        npw = 0
        for hf in range(2):
            S.label = f'cv_conv{hf}'
            for oc in range(KC):
                dg = diag[oc % 2]
                for k in range(31):
                    S.ts(dg[:, k, :], self.identb[:], wdw(oc, k), ALU.mult)
                for t2 in range(2):
                    tt = hf * 2 + t2
                    pc = self.ps[4 + (oc * 2 + t2) % 2]
                    for k in range(31):
                        S.mm(pc[:], dg[:, k, :], u[:, oc, tt * TT + k:tt * TT + k + TT],
                             start=(k == 0), stop=(k == 30), lazy=True)
                    S.act(y[:, oc, t2 * TT:(t2 + 1) * TT], pc[:], AF.Identity, bias=bdw(oc))
            S.label = f'cv_ln{hf}'
            for t2 in range(2):
                ysl = slice(t2 * TT, (t2 + 1) * TT)
                psA = self.ps[6] if t2 == 0 else self.ps[2]
                psB = self.ps[7] if t2 == 0 else self.ps[3]
                for kc in range(KC):
                    qa = self.sq[kc % 2]
                    qb = self.sqb[kc % 2]
                    S.copy(qa[:], y[:, kc, ysl], eng="act")
                    S.act(qb[:], y[:, kc, ysl], AF.Square)
                    S.mm(psA[:], self.onesb[:], qa[:], start=(kc == 0), stop=(kc == KC - 1))
                    S.mm(psB[:], self.onesb[:], qb[:], start=(kc == 0), stop=(kc == KC - 1))
                m = self.rstd[0]
                r = self.rstd[1]
                S.act(m[:], psA[:], AF.Identity, scale=1.0 / D)
                S.tt(r[:], m[:], m[:], ALU.mult)
                S.stt(r[:], psB[:], 1.0 / D, r[:], ALU.mult, ALU.subtract)
                S.act(r[:], r[:], AF.Ln, bias=EPS)
                S.act(r[:], r[:], AF.Exp, scale=-0.5)
                for kc in range(KC):
                    xn = self.xn[kc % 2]
                    S.tt(xn[:], y[:, kc, ysl], m[:], ALU.subtract)
                    S.tt(xn[:], xn[:], r[:], ALU.mult)
                    S.act(u2[:, kc, ysl], xn[:], AF.Silu, scale=lng(kc), bias=lnb(kc))
            S.label = f'cv_pw{hf}'
            for oc in range(KC):
                wp = wpw[npw % 2]
                npw += 1
                S.dma("pool", wp, self.d["cmpw"][a, oc].rearrange("p (k n) -> p k n", k=KC))
                for t2 in range(2):
                    tt = hf * 2 + t2
                    sl = slice(tt * TT, (tt + 1) * TT)
                    po = self.ps[(oc * 2 + t2) % 2]
                    for kc in range(KC):
                        S.mm(po[:], wp[:, kc, :], u2[:, kc, t2 * TT:(t2 + 1) * TT],
                             start=(kc == 0), stop=(kc == KC - 1), lazy=True)
                    tmp = self.sg[(oc * 2 + t2) % 2]
                    S.act(tmp[:], po[:], AF.Identity, scale=gate[:, oc:oc + 1], bias=self.cbg[:, oc:oc + 1])
                    S.tt(self.xs[:, oc, sl], self.xs[:, oc, sl], tmp[:], ALU.add)

    def _dn_mixer(self, i, s):
        S = self.S
        m = i // 2
        j = 1
        S.label = 'dn_norm'
        self._norm_mod(i, s, j)
        ps = self.ps
        gate = self.msc[:, i, s, j, 2, :]
        sm = self.dnsm
        sconv = lambda c24, k: sm[:, m * 97 + c24 * 4 + k:m * 97 + c24 * 4 + k + 1]
        og = sm[:, m * 97 + 96:m * 97 + 97]
        NTL = 16
        RG, R2, RC, RV, RW, RK, RU = 0, 4096, 16384, 20544, 32832, 41024, 53312
        gz = self.scr_b16(RG, [T])
        tmpf = lambda k: self.scr_f32(R2 + k * 2048, [TT])
        cf = self.scr_f32(R2, [T])
        dgs = self.scr_b16(R2 + 4 * 2048, [4, 128])
        cin = self.scr_b16(RC, [3 + T])
        vb = self.scr_b16(RV, [NTL, 128])
        kbg = self.scr_b16(RV + 4096, [NTL, 128])
        kd = self.scr_b16(RV + 8192, [NTL, 128])
        wqk = self.scr_b16(RW, [KC, 256])
        wvz = self.scr_b16(RW + 4096, [KC, 256])
        Rb = self.scr_b16(RW, [T])
        intraT = self.scr_b16(RW + 4096, [T])
        kTb = self.scr_b16(RK, [T])
        qTb = self.scr_b16(RK + 4096, [T])
        qgT = self.scr_b16(RK + 8192, [T])
        u = self.scr_f32(RU, [NTL, 128])
        wT = self.scr_b16(RU + 8192, [T])
        tmpr = lambda k: self.slots[:, k * TT:(k + 1) * TT]
        tmpR = tmpr
        S.label = 'dn_gates'
        wab = self.scr_b16(RW, [KC, 16])
        S.dma("pool", wab, self.d["dnwab"][m].rearrange("p (k n) -> p k n", k=KC))
        a_sb = self.scr_f32(RC, [T])
        b_sb = self.scr_f32(RC + 8192, [T])
        gc = self.scr_f32(RK, [T])
        cmk = self.scr_f32(RK + 8192, [T])
        S.dma("sp", cmk[0:8, :], self.d["cmask"][:, :])
        S.act(self.nexpa[:], self.dnhp[:, 2 * m:2 * m + 1], AF.Exp)
        S.ts(self.nexpa[:], self.nexpa[:], -1.0, ALU.mult)
        for tt in range(NT):
            sl = slice(tt * TT, (tt + 1) * TT)
            pa = ps[0]
            pb = ps[1]
            for kc in range(KC):
                S.mm(pa[0:8, :], wab[:, kc, 0:8], self.hb[:, kc, sl], start=(kc == 0), stop=(kc == KC - 1), lazy=True)
            for kc in range(KC):
                S.mm(pb[0:8, :], wab[:, kc, 8:16], self.hb[:, kc, sl], start=(kc == 0), stop=(kc == KC - 1), lazy=True)
            S.act(a_sb[0:8, sl], pa[0:8, :], AF.Exp, bias=self.dnhp[:, 2 * m + 1:2 * m + 2])
            S.act(a_sb[0:8, sl], a_sb[0:8, sl], AF.Ln, bias=1.0)
            S.act(b_sb[0:8, sl], pb[0:8, :], AF.Sigmoid)
        S.ts(a_sb[0:8, :], a_sb[0:8, :], self.nexpa[:, 0:1], ALU.mult)
        S.op("dve", lambda e: e.tensor_tensor_scan(out=gc[0:8, :], data0=cmk[0:8, :], data1=a_sb[0:8, :],
                                                   initial=0.0, op0=ALU.mult, op1=ALU.add),
             reads=[cmk[0:8, :], a_sb[0:8, :]], writes=[gc[0:8, :]])
        for n in range(NTL):
            S.transpose(ps[2][:, n * 8:(n + 1) * 8], gc[0:8, n * 128:(n + 1) * 128], self.identf[0:8, 0:8])
            S.transpose(ps[3][:, n * 8:(n + 1) * 8], b_sb[0:8, n * 128:(n + 1) * 128], self.identf[0:8, 0:8])
        S.copy(self.gtok[:], ps[2][:, 0:128])
        S.copy(self.btok[:], ps[3][:, 0:128], eng="act")
        S.mm(ps[4][:, 0:128], self.sel127[:], self.gtok[:])
        S.tt(self.ekd[:], ps[4][:, 0:128], self.gtok[:], ALU.subtract)
        S.act(self.ekd[:], self.ekd[:], AF.Exp)
        S.act(self.bkg[:], self.gtok[:], AF.Exp)
        S.tt(self.bkg[:], self.bkg[:], self.btok[:], ALU.mult)
        mUi = self.maskc[:, 0:128].unsqueeze(1).to_broadcast([128, 4, 128])
        mLs = self.maskc[:, 256:384].unsqueeze(1).to_broadcast([128, 4, 128])
        idb4 = self.identf[:, :].unsqueeze(1).to_broadcast([128, 4, 128])
        v4 = lambda ap: ap.rearrange("p (a b) -> p a b", a=4)
        nbank = [0]

        def project(wsrc, c0, evac):
            for tt in range(NT):
                sl = slice(tt * TT, (tt + 1) * TT)
                pp = ps[nbank[0] % 4]
                nbank[0] += 1
                for kc in range(KC):
                    S.mm(pp[:], wsrc[:, kc, c0:c0 + 128], self.hb[:, kc, sl], start=(kc == 0), stop=(kc == KC - 1), lazy=True)
                evac(tt, sl, pp)

        def conv_silu(c24):
            for k in range(4):
                S.ts(dgs[:, k, :], self.identb[:], sconv(c24, k), ALU.mult)
            for tt in range(NT):
                sl = slice(tt * TT, (tt + 1) * TT)
                pc = ps[4 + tt % 2]
                for k in range(4):
                    S.mm(pc[:], dgs[:, k, :], cin[:, tt * TT + k:tt * TT + k + TT], start=(k == 0), stop=(k == 3), lazy=True)
                S.act(cf[:, sl], pc[:], AF.Silu)

        def to_cin(tt, sl, pp):
            S.copy(cin[:, 3 + tt * TT:3 + (tt + 1) * TT], pp[:], eng=("act" if tt % 2 == 0 else "dve"))

        def l2norm(post):
            for tt in range(NT):
                sl = slice(tt * TT, (tt + 1) * TT)
                sq = self.sq[tt % 2]
                pst = ps[6 + tt % 2]
                S.act(sq[:], cf[:, sl], AF.Square)
                S.mm(pst[:], self.onesb[:], sq[:])
                r = self.rstd[tt % 2]
                S.act(r[:], pst[:], AF.Ln, bias=EPS)
                S.act(r[:], r[:], AF.Exp, scale=-0.5)
                post(sl, r)

        for hd in range(8):
            col = lambda n: n * 8 + hd
            S.dma("pool", wqk, self.d["dnwin"][m, hd, 0].rearrange("p (k n) -> p k n", k=KC))
            S.dma("pool", wvz, self.d["dnwin"][m, hd, 1].rearrange("p (k n) -> p k n", k=KC))
            S.memset(cin[:, 0:3], 0.0)
            S.label = f'h{hd}_z'
            project(wvz, 128, lambda tt, sl, pp: S.act(gz[:, sl], pp[:], AF.Silu))
            S.label = f'h{hd}_v'
            project(wvz, 0, to_cin)
            conv_silu(16 + hd)
            for bt in range(4):
                pv_ = ps[2 + bt % 2]
                for t in range(4):
                    n = bt * 4 + t
                    S.transpose(pv_[:, t * 128:(t + 1) * 128], cf[:, n * 128:(n + 1) * 128], self.identf[:])
                for t in range(4):
                    n = bt * 4 + t
                    if t % 2 == 0:
                        S.ts(vb[:, n, :], pv_[:, t * 128:(t + 1) * 128], self.btok[:, col(n):col(n) + 1], ALU.mult)
                    else:
                        S.act(vb[:, n, :], pv_[:, t * 128:(t + 1) * 128], AF.Identity, scale=self.btok[:, col(n):col(n) + 1])
            S.label = f'h{hd}_k'
            project(wqk, 128, to_cin)
            conv_silu(8 + hd)

            def post_k(sl, r):
                S.tt(cf[:, sl], cf[:, sl], r[:], ALU.mult)
                S.copy(kTb[:, sl], cf[:, sl], eng="act")
            l2norm(post_k)
            for bt in range(4):
                pk = ps[bt % 2]
                for t in range(4):
                    n = bt * 4 + t
                    S.transpose(pk[:, t * 128:(t + 1) * 128], cf[:, n * 128:(n + 1) * 128], self.identf[:])
                for t in range(4):
                    n = bt * 4 + t
                    S.ts(kbg[:, n, :], pk[:, t * 128:(t + 1) * 128], self.bkg[:, col(n):col(n) + 1], ALU.mult)
                    S.act(kd[:, n, :], pk[:, t * 128:(t + 1) * 128], AF.Identity, scale=self.ekd[:, col(n):col(n) + 1])
            S.label = f'h{hd}_q'
            project(wqk, 0, to_cin)
            conv_silu(hd)

            def post_q(sl, r):
                S.stt(cf[:, sl], cf[:, sl], float(128 ** -0.5), r[:], ALU.mult, ALU.mult)
                S.copy(qTb[:, sl], cf[:, sl], eng="act")
            l2norm(post_q)
            for bt in range(4):
                bsl = slice(bt * TT, (bt + 1) * TT)
                dG = tmpf(4)
                EG = tmpf(5)
                for t in range(4):
                    n = bt * 4 + t
                    S.ts(dG[:, t * 128:(t + 1) * 128], self.onesf[:], self.gtok[:, col(n):col(n) + 1], ALU.mult)
                for t in range(4):
                    tsl = slice(t * 128, (t + 1) * 128)
                    S.transpose(ps[0][:, tsl], dG[:, tsl], self.identf[:])
                S.act(EG, ps[0][:], AF.Exp)
                S.copy(self.egl[:, bt * 4:(bt + 1) * 4], EG[:, 127::128])
                S.tt(qgT[:, bsl], cf[:, bsl], EG, ALU.mult)
            wo_h = self.scr_b16(RC, [D])
            S.dma("pool", wo_h, self.d["dnwout"][m, hd])
            oT = [self.ot0[:], self.scr_f32(RC + 2048, [TT])]
            on = self.scr_b16(R2 + 4 * 2048, [T])
            S.memset(self.Sst[:], 0.0)
            S.memset(self.Sb[0][:], 0.0)

            def s4_gen(bt):
                bsl = slice(bt * TT, (bt + 1) * TT)
                dG = tmpf(0)
                for t in range(4):
                    n = bt * 4 + t
                    S.ts(dG[:, t * 128:(t + 1) * 128], self.onesf[:], self.gtok[:, col(n):col(n) + 1], ALU.mult)
                for t in range(4):
                    tsl = slice(t * 128, (t + 1) * 128)
                    S.transpose(ps[0][:, tsl], dG[:, tsl], self.identf[:])
                for t in range(4):
                    n = bt * 4 + t
                    tsl = slice(t * 128, (t + 1) * 128)
                    nsl = slice(n * 128, (n + 1) * 128)
                    S.mm(ps[1][:, tsl], kTb[:, nsl], kTb[:, nsl])
                yield
                EU = tmpf(1)
                EL = tmpf(2)
                for t in range(4):
                    n = bt * 4 + t
                    tsl = slice(t * 128, (t + 1) * 128)
                    gcol = self.gtok[:, col(n):col(n) + 1]
                    S.ts(EU[:, tsl], ps[0][:, tsl], gcol, ALU.subtract, 0.0, ALU.min)
                    S.ts(EL[:, tsl], ps[0][:, tsl], gcol, ALU.subtract, 0.0, ALU.max)
                S.act(EU, EU, AF.Exp)
                S.act(EL, EL, AF.Exp, scale=-1.0)
                for t in range(4):
                    n = bt * 4 + t
                    tsl = slice(t * 128, (t + 1) * 128)
                    nsl = slice(n * 128, (n + 1) * 128)
                    S.mm(ps[0][:, tsl], kTb[:, nsl], qTb[:, nsl])
                yield
                S.tt(v4(EL), v4(EL), mLs, ALU.mult)
                S.tt(v4(EU), v4(EU), mUi, ALU.mult)
                Af = tmpf(3)
                for t in range(4):
                    n = bt * 4 + t
                    tsl = slice(t * 128, (t + 1) * 128)
                    S.stt(Af[:, tsl], ps[1][:, tsl], self.btok[:, col(n):col(n) + 1], EL[:, tsl], ALU.mult, ALU.mult)
                for t in range(4):
                    tsl = slice(t * 128, (t + 1) * 128)
                    S.transpose(ps[1][:, tsl], Af[:, tsl], self.identf[:])
                S.tt(intraT[:, bsl], ps[0][:], EU, ALU.mult)
                yield
                slotA = [tmpR(0), tmpR(1)]
                slotU = [tmpR(2), tmpR(3)]
                P = tmpR(4)
                Pf = tmpr(4)
                S.copy(slotA[0], Af, eng="dve")
                S.copy(slotU[0], ps[1][:], eng="dve")
                S.tt(v4(P), idb4, v4(ps[1][:]), ALU.subtract)
                pA, pU, pP = ps[2], ps[3], ps[4]

                def square(l):
                    A_c, U_c = slotA[l % 2], slotU[l % 2]
                    for t in range(4):
                        tsl = slice(t * 128, (t + 1) * 128)
                        S.mm(pA[:, tsl], U_c[:, tsl], A_c[:, tsl])
                    if l + 1 < 6:
                        for t in range(4):
                            tsl = slice(t * 128, (t + 1) * 128)
                            S.mm(pU[:, tsl], A_c[:, tsl], U_c[:, tsl])

                def evac(l):
                    S.copy(slotA[(l + 1) % 2], pA[:], eng="act")
                    if l + 1 < 6:
                        S.copy(slotU[(l + 1) % 2], pU[:], eng="dve")
                square(0)
                evac(0)
                yield
                for l in range(1, 7):
                    A_l = slotA[l % 2]
                    if l < 6:
                        square(l)
                    for t in range(4):
                        tsl = slice(t * 128, (t + 1) * 128)
                        S.mm(pP[:, tsl], A_l[:, tsl], P[:, tsl])
                    if l < 6:
                        evac(l)
                        S.tt(P, pP[:], Pf, ALU.add)
                    else:
                        S.tt(Rb[:, bsl], pP[:], Pf, ALU.add)
                    yield
                for t in range(4):
                    n = bt * 4 + t
                    tsl = slice(t * 128, (t + 1) * 128)
                    nsl = slice(n * 128, (n + 1) * 128)
                    S.mm(ps[0][:, tsl], Rb[:, nsl], vb[:, n, :])
                    S.mm(ps[1][:, tsl], kbg[:, n, :], Rb[:, nsl])
                S.copy(u[:, bt * 4:(bt + 1) * 4, :].rearrange("p a b -> p (a b)"), ps[0][:], eng="act")
                S.copy(wT[:, bsl], ps[1][:])
                yield

            def scan_gen(bt):
                bsl = slice(bt * TT, (bt + 1) * TT)
                po = ps[6]
                for t in range(4):
                    n = bt * 4 + t
                    nsl = slice(n * 128, (n + 1) * 128)
                    tsl = slice(t * 128, (t + 1) * 128)
                    Sb = self.Sb[n % 2]
                    Sbn = self.Sb[(n + 1) % 2]
                    vn = self.vnew[n % 2]
                    S.mm(ps[5][:, 0:128], wT[:, nsl], Sb[:])
                    S.mm(po[:, tsl], Sb[:], qgT[:, nsl], start=True, stop=False)
                    yield
                    S.tt(vn[:], u[:, n, :], ps[5][:, 0:128], ALU.subtract)
                    S.mm(po[:, tsl], vn[:], intraT[:, nsl], start=False, stop=True)
                    S.mm(ps[5][:, 128:256], kd[:, n, :], vn[:])
                    S.stt(Sbn[:], self.Sst[:], self.egl[:, n:n + 1], ps[5][:, 128:256], ALU.mult, ALU.add)
                    S.stt(self.Sst[:], self.Sst[:], self.egl[:, n:n + 1], ps[5][:, 128:256], ALU.mult, ALU.add)
                    yield
                sq = self.sq[bt % 2]
                pst = ps[7]
                S.act(sq[:], po[:], AF.Square)
                S.mm(pst[:], self.onesb[:], sq[:])
                r = self.rstd[bt % 2]
                S.act(r[:], pst[:], AF.Ln, scale=1.0 / 128, bias=EPS)
                S.act(r[:], r[:], AF.Exp, scale=-0.5)
                ot = oT[bt % 2]
                S.tt(ot, po[:], r[:], ALU.mult)
                S.stt(on[:, bsl], ot, og, gz[:, bsl], ALU.mult, ALU.mult)
                yield

            def s6_gen(tt):
                sl = slice(tt * TT, (tt + 1) * TT)
                for oc in range(KC):
                    pp = ps[7]
                    S.mm(pp[:], wo_h[:, oc * 128:(oc + 1) * 128], on[:, sl])
                    S.stt(self.xs[:, oc, sl], pp[:], gate[:, oc:oc + 1], self.xs[:, oc, sl], ALU.mult, ALU.add)
                    yield

            for k in range(6):
                S.label = f'h{hd}_pipe{k}'
                gens = []
                if k < 4:
                    gens.append(s4_gen(k))
                if 1 <= k <= 4:
                    gens.append(scan_gen(k - 1))
                if 2 <= k <= 5:
                    gens.append(s6_gen(k - 2))
                while gens:
                    for g in list(gens):
                        try:
                            next(g)
                        except StopIteration:
                            gens.remove(g)

    def _finish(self, s):
        S = self.S
        toks = []
        if self.do_final:
            self._norm(lambda kc: self.fg[:, kc:kc + 1], lambda kc: 0.0,
                       lambda kc, tt: self.xs[:, kc, tt * TT:(tt + 1) * TT])
        for kc in range(KC):
            toks.append(S.dma("sp", self.out[s, kc * 128:(kc + 1) * 128, :], self.xs[:, kc, :]))
        return toks


def _fm(v):
    v = np.asarray(v, np.float32)
    lead = v.shape[:-1]
    return np.ascontiguousarray(np.moveaxis(v.reshape(lead + (KC, 128)), -1, 0))


def prep_shared(inp, layers):
    sh = {}
    L = list(layers)
    sh["ng"] = _fm(inp["norm_g"]).reshape(128, DEPTH * 3 * KC)
    wa = np.asarray(inp["w_ada"], np.float32)
    wa = wa.reshape(DEPTH, KC, 128, 72, 128).transpose(0, 3, 2, 1, 4)
    sh["wada"] = np.ascontiguousarray(wa[L]).reshape(len(L), 72, 128, KC * 128)
    ba = np.asarray(inp["b_ada"], np.float32).reshape(DEPTH, 72, 128)
    sh["bada"] = np.ascontiguousarray(ba.transpose(2, 0, 1)).reshape(128, DEPTH * 72)
    wi = np.asarray(inp["w_ffn_in"], np.float32)
    wi = wi.reshape(DEPTH, 2, KC, 128, 2, FC, 128)
    wi = wi.transpose(0, 1, 5, 3, 2, 4, 6)
    sh["win"] = np.ascontiguousarray(wi[L]).reshape(len(L), 2, FC, 128, KC * 256)
    wo = np.asarray(inp["w_ffn_out"], np.float32)
    wo = wo.reshape(DEPTH, 2, 2, 11, 128, KC, 128)
    wo = wo.transpose(0, 1, 2, 5, 4, 3, 6)
    sh["wout"] = np.ascontiguousarray(wo[L]).reshape(len(L), 2, 2, KC, 128, 11 * 128)
    sh["fg"] = _fm(inp["final_g"]).reshape(128, KC)
    wg = np.asarray(inp["cm_w_glu"], np.float32).reshape(2, KC, 128, 2, KC, 128)
    sh["cmglu"] = np.ascontiguousarray(wg.transpose(0, 4, 2, 1, 3, 5)).reshape(2, KC, 128, KC * 256)
    wp = np.asarray(inp["cm_w_pw"], np.float32).reshape(2, KC, 128, KC, 128)
    sh["cmpw"] = np.ascontiguousarray(wp.transpose(0, 3, 2, 1, 4)).reshape(2, KC, 128, KC * 128)
    sm = np.zeros((128, 2, 296), np.float32)
    bg = _fm(np.asarray(inp["cm_b_glu"], np.float32).reshape(2, 2, D))
    sm[:, :, 0:8] = bg[:, :, 0, :]
    sm[:, :, 8:16] = bg[:, :, 1, :]
    wd = _fm(inp["cm_w_dw"])
    sm[:, :, 16:264] = wd.transpose(0, 1, 3, 2).reshape(128, 2, 248)
    sm[:, :, 264:272] = _fm(inp["cm_b_dw"])
    sm[:, :, 272:280] = _fm(inp["cm_ln_g"])
    sm[:, :, 280:288] = _fm(inp["cm_ln_b"])
    sm[:, :, 288:296] = _fm(inp["cm_b_pw"])
    sh["cmsm"] = sm.reshape(128, 2 * 296)
    W = 1024
    dw = np.asarray(inp["dn_w_in"], np.float32)
    parts = [dw[:, :, c * W:(c + 1) * W].reshape(2, KC, 128, 8, 128) for c in range(4)]
    qk = np.stack([parts[0], parts[1]], axis=4)
    vz = np.stack([parts[2], parts[3]], axis=4)
    both = np.stack([qk, vz], axis=4)
    sh["dnwin"] = np.ascontiguousarray(both.transpose(0, 3, 4, 2, 1, 5, 6)).reshape(2, 8, 2, 128, KC * 256)
    ab = dw[:, :, 4 * W:4 * W + 16].reshape(2, KC, 128, 16)
    sh["dnwab"] = np.ascontiguousarray(ab.transpose(0, 2, 1, 3)).reshape(2, 128, KC * 16)
    sh["dnwout"] = np.ascontiguousarray(np.asarray(inp["dn_w_out"], np.float32).reshape(2, 8, 128, D))
    dsm = np.zeros((128, 2, 97), np.float32)
    sc = np.asarray(inp["dn_w_sconv"], np.float32).reshape(2, 4, 24, 128)
    dsm[:, :, 0:96] = sc.transpose(3, 0, 2, 1).reshape(128, 2, 96)
    dsm[:, :, 96] = np.asarray(inp["dn_o_g"], np.float32).T
    sh["dnsm"] = dsm.reshape(128, 2 * 97)
    hp = np.zeros((8, 4), np.float32)
    for mm_ in range(2):
        hp[:, 2 * mm_] = np.asarray(inp["dn_a_log"], np.float32)[mm_]
        hp[:, 2 * mm_ + 1] = np.asarray(inp["dn_dt_bias"], np.float32)[mm_]
    sh["dnhp"] = hp
    pi = np.arange(128)[:, None]
    fi = np.arange(128)[None, :]
    sh["maskc"] = np.concatenate([(fi >= pi), (fi > pi), (pi > fi)], axis=1).astype(np.float32)
    s127 = np.zeros((128, 128), np.float32)
    s127[127, :] = 1.0
    sh["sel127"] = s127
    cm = np.ones((8, T), np.float32)
    cm[:, ::128] = 0.0
    sh["cmask"] = cm
    sh["identf"] = np.eye(128, dtype=np.float32)
    return sh


def prep_core(inp, c):
    m = {}
    x = np.asarray(inp["x"], np.float32)
    m["xT"] = np.ascontiguousarray(x[2 * c:2 * c + 2].transpose(0, 2, 1))
    cc = np.asarray(inp["c"], np.float32)[2 * c:2 * c + 2]
    m["cT"] = np.ascontiguousarray(cc.reshape(2, KC, 128).transpose(2, 1, 0)).reshape(128, KC * 2)
    return m


_PROG_CACHE = {}


def run(inputs, stages=None, nseq=2, do_final=True, ncores=NCORES, debug=False):
    key = (None if stages is None else tuple(stages), nseq, do_final, debug)
    if key not in _PROG_CACHE:
        _PROG_CACHE[key] = Prog(stages, nseq, do_final, debug)
    prog = _PROG_CACHE[key]
    sh = prep_shared(inputs, prog.layers)
    in_maps = []
    for c in range(ncores):
        m = dict(sh)
        m.update(prep_core(inputs, c))
        in_maps.append(m)
    res = run_bass_kernel_spmd(prog.nc, in_maps, core_ids=list(range(ncores)))
    if debug:
        global LAST_DBG
        LAST_DBG = np.asarray(res.results[0]["dbg"])
    B = 2 * ncores
    out = np.empty((B, T, D), np.float32)
    for c in range(ncores):
        o = np.asarray(res.results[c]["outT"])
        out[2 * c:2 * c + 2] = o.transpose(0, 2, 1)
    return out


def kernel(**inputs):
    return run(inputs)
```
